# Optimizing a Trainium2 kernel written in Bass

```python
import math
import jax, jax.numpy as jnp
from jax import lax
import numpy as np

D_MODEL = 1024
BATCH = 4
SEQ = 4096
DEPTH = 2

HEAD_DIM = 64
SB_HEADS = 8
SW_HEADS = 8
SW_KV_HEADS = 2
SW_GROUP = SW_HEADS // SW_KV_HEADS
WINDOW = 128
BLOCK = 128
D_FF = 4 * D_MODEL
PLE_DIM = 256
N_BUCKETS = 32
MAX_DISTANCE = 128
EPS = 1e-6

SB_W = SB_HEADS * HEAD_DIM
SW_QW = SW_HEADS * HEAD_DIM
SW_KVW = SW_KV_HEADS * HEAD_DIM
IN_COLS = 3 * SB_W + SW_QW + 2 * SW_KVW + 2 * D_MODEL

kernel_name = "stick_breaking_swa_sink_hybrid_block"


def rmsnorm(x, g):
    xf = x.astype(jnp.float32)
    r = lax.rsqrt(jnp.mean(xf * xf, axis=-1, keepdims=True) + EPS)
    return (xf * r).astype(x.dtype) * g


def t5_causal_bucket(dist):
    max_exact = N_BUCKETS // 2
    d = jnp.maximum(dist, 0)
    df = jnp.maximum(d, 1).astype(jnp.float32)
    large = max_exact + (jnp.log(df / max_exact) / math.log(MAX_DISTANCE / max_exact)
                         * (N_BUCKETS - max_exact)).astype(jnp.int32)
    large = jnp.minimum(large, N_BUCKETS - 1)
    return jnp.where(d < max_exact, d, large)


def stick_breaking_attention(q, k, v):
    b_, s_, h_, dh = q.shape
    nb = s_ // BLOCK
    scale = dh ** -0.5
    qb = q.reshape(b_, nb, BLOCK, h_, dh).transpose(1, 0, 3, 2, 4)
    kpos = jnp.arange(s_)

    def one_block(args):
        qblk, n = args
        z = jnp.einsum('bhqd,bshd->bhqs', qblk, k).astype(jnp.float32) * scale
        qpos = n * BLOCK + jnp.arange(BLOCK)
        causal = kpos[None, :] < qpos[:, None]
        log_beta = jax.nn.log_sigmoid(z)
        log_1mb = jnp.where(causal, jax.nn.log_sigmoid(-z), 0.0)
        suffix = lax.cumsum(log_1mb, axis=3, reverse=True) - log_1mb
        a = jnp.where(causal, jnp.exp(log_beta + suffix), 0.0)
        return jnp.einsum('bhqs,bshd->bqhd', a.astype(v.dtype), v)

    o = lax.map(one_block, (qb, jnp.arange(nb)))
    return o.transpose(1, 0, 2, 3, 4).reshape(b_, s_, h_ * dh)


def sliding_window_sink_attention(q, k, v, sinks, bias):
    b_, s_ = q.shape[:2]
    nb = s_ // BLOCK
    scale = HEAD_DIM ** -0.5
    qb = q.reshape(b_, nb, BLOCK, SW_KV_HEADS, SW_GROUP, HEAD_DIM)
    pad = jnp.zeros((b_, BLOCK, SW_KV_HEADS, HEAD_DIM), k.dtype)
    kp = jnp.concatenate([pad, k], axis=1).reshape(b_, nb + 1, BLOCK, SW_KV_HEADS, HEAD_DIM)
    vp = jnp.concatenate([pad, v], axis=1).reshape(b_, nb + 1, BLOCK, SW_KV_HEADS, HEAD_DIM)
    kb = jnp.concatenate([kp[:, :-1], kp[:, 1:]], axis=2)
    vb = jnp.concatenate([vp[:, :-1], vp[:, 1:]], axis=2)
    s = jnp.einsum('bnqkgd,bnskd->bnkgqs', qb, kb).astype(jnp.float32) * scale + bias
    i = jnp.arange(BLOCK)[:, None]
    j = jnp.arange(2 * BLOCK)[None, :]
    dist = BLOCK + i - j
    band = (dist >= 0) & (dist < WINDOW)
    key_abs = (jnp.arange(nb)[:, None, None] - 1) * BLOCK + j[None]
    valid = band[None] & (key_abs >= 0)
    s = jnp.where(valid[None, :, None, None], s, -jnp.inf)
    sink = sinks.astype(jnp.float32).reshape(SW_KV_HEADS, SW_GROUP)[None, None, :, :, None, None]
    m = jnp.maximum(jnp.max(s, axis=-1, keepdims=True), sink)
    e = jnp.exp(s - m)
    denom = jnp.sum(e, axis=-1, keepdims=True) + jnp.exp(sink - m)
    pr = (e / denom).astype(v.dtype)
    o = jnp.einsum('bnkgqs,bnskd->bnqkgd', pr, vb)
    return o.reshape(b_, s_, SW_HEADS * HEAD_DIM)


def window_bias(rel_bias):
    i = jnp.arange(BLOCK)[:, None]
    j = jnp.arange(2 * BLOCK)[None, :]
    dist = BLOCK + i - j
    bias = rel_bias[t5_causal_bucket(dist)]
    return bias.transpose(2, 0, 1).reshape(SW_KV_HEADS, SW_GROUP, BLOCK, 2 * BLOCK)


def setup_inputs(seed: int = 0) -> dict:
    key = jax.random.key(seed)
    ks = jax.random.split(key, 20)
    f32 = jnp.float32

    def nrm(k, shape, fan_in):
        return jax.random.normal(k, shape, f32) * (fan_in ** -0.5)

    return {
        "x": jax.random.normal(ks[0], (BATCH, SEQ, D_MODEL), f32),
        "p": jax.random.normal(ks[1], (DEPTH, BATCH, SEQ, PLE_DIM), f32),
        "w_in": nrm(ks[2], (DEPTH, D_MODEL, IN_COLS), D_MODEL),
        "w_up_a": nrm(ks[3], (DEPTH, SB_W, D_MODEL), SB_W),
        "w_up_b": nrm(ks[4], (DEPTH, SW_QW, D_MODEL), SW_QW),
        "w_o": nrm(ks[5], (DEPTH, D_MODEL, D_MODEL), D_MODEL),
        "w_ff1": nrm(ks[6], (DEPTH, D_MODEL, D_FF), D_MODEL),
        "w_ff2": nrm(ks[7], (DEPTH, D_FF, D_MODEL), D_FF),
        "w_pe": nrm(ks[8], (DEPTH, PLE_DIM, D_MODEL), PLE_DIM),
        "w_pg": nrm(ks[9], (DEPTH, D_MODEL, D_MODEL), D_MODEL),
        "g_mix": 1.0 + 0.01 * jax.random.normal(ks[10], (DEPTH, D_MODEL), f32),
        "g_mlp": 1.0 + 0.01 * jax.random.normal(ks[11], (DEPTH, D_MODEL), f32),
        "g_pe": 1.0 + 0.01 * jax.random.normal(ks[12], (DEPTH, D_MODEL), f32),
        "g_final": 1.0 + 0.01 * jax.random.normal(ks[13], (D_MODEL,), f32),
        "sinks": 0.5 * jax.random.normal(ks[14], (DEPTH, SW_HEADS), f32),
        "rel_bias": 0.5 * jax.random.normal(ks[15], (N_BUCKETS, SW_HEADS), f32),
    }


def reference(x, p, w_in, w_up_a, w_up_b, w_o, w_ff1, w_ff2, w_pe, w_pg,
              g_mix, g_mlp, g_pe, g_final, sinks, rel_bias):
    b_, s_, _ = x.shape
    bias = window_bias(rel_bias)
    o1 = SB_W
    o2 = o1 + SB_W
    o3 = o2 + SB_W
    o4 = o3 + SW_QW
    o5 = o4 + SW_KVW
    o6 = o5 + SW_KVW
    o7 = o6 + D_MODEL
    for i in range(DEPTH):
        h = rmsnorm(x, g_mix[i])
        proj = h @ w_in[i]
        q_a = proj[..., :o1].reshape(b_, s_, SB_HEADS, HEAD_DIM)
        k_a = proj[..., o1:o2].reshape(b_, s_, SB_HEADS, HEAD_DIM)
        v_a = proj[..., o2:o3].reshape(b_, s_, SB_HEADS, HEAD_DIM)
        q_b = proj[..., o3:o4].reshape(b_, s_, SW_HEADS, HEAD_DIM)
        k_b = proj[..., o4:o5].reshape(b_, s_, SW_KV_HEADS, HEAD_DIM)
        v_b = proj[..., o5:o6].reshape(b_, s_, SW_KV_HEADS, HEAD_DIM)
        gate_a = proj[..., o6:o7]
        gate_b = proj[..., o7:]
        y_a = stick_breaking_attention(q_a, k_a, v_a) @ w_up_a[i]
        y_b = sliding_window_sink_attention(q_b, k_b, v_b, sinks[i], bias) @ w_up_b[i]
        merged = jax.nn.sigmoid(gate_a) * y_a + jax.nn.sigmoid(gate_b) * y_b
        x = x + merged @ w_o[i]
        h = rmsnorm(x, g_mlp[i])
        x = x + jnp.square(jax.nn.relu(h @ w_ff1[i])) @ w_ff2[i]
        pe = p[i] @ w_pe[i]
        x = x + pe * jax.nn.sigmoid(rmsnorm(x, g_pe[i]) @ w_pg[i])
    return rmsnorm(x, g_final)
```

```python
import contextlib
import numpy as np
import ml_dtypes
import concourse.bass as bass
import concourse.mybir as mybir
from concourse.bass_utils import run_bass_kernel_spmd

F32 = mybir.dt.float32
BF16 = mybir.dt.bfloat16
AF = mybir.ActivationFunctionType
ALU = mybir.AluOpType
AX = mybir.AxisListType

ENGINES = ("pe", "act", "dve", "pool", "sp")
NT = 2048
NG = 4
NEG = -30000.0
KEEP_WARM = 2
KEEP_WARM_A = 4
KEEP_WARM_SW = 6
EPS = 1e-6


class Res:
    __slots__ = ("name", "last_write", "reads", "dma_sem", "dma_count")

    def __init__(self, name):
        self.name = name
        self.last_write = None
        self.reads = []
        self.dma_sem = None
        self.dma_count = 0


class Prog:
    def __init__(self, nc):
        self.nc = nc
        self.q = {e: [] for e in ENGINES}
        self.cnt = {e: 0 for e in ENGINES}
        self.seen = {e: {} for e in ENGINES}
        self.dma_res = []

    def _collect(self, eng, reads, writes):
        waits = {}

        def add(tok):
            if tok is None:
                return
            k, v = tok
            if waits.get(k, 0) < v:
                waits[k] = v
        for r in reads:
            add(r.last_write)
        for w in writes:
            add(w.last_write)
            for t in w.reads:
                add(t)
        out = []
        seen = self.seen[eng]
        for k, v in waits.items():
            if k == eng and eng == "pe":
                continue
            if seen.get(k, 0) >= v:
                continue
            seen[k] = v
            out.append((k, v))
        return out

    @staticmethod
    def _commit(tok, reads, writes):
        for r in reads:
            r.reads.append(tok)
            if len(r.reads) > 64:
                best = {}
                for k, v in r.reads:
                    if best.get(k, 0) < v:
                        best[k] = v
                r.reads = list(best.items())
        for w in writes:
            w.last_write = tok
            w.reads = []

    def op(self, eng, fn, reads=(), writes=()):
        waits = self._collect(eng, reads, writes)
        self.cnt[eng] += 1
        tok = (eng, self.cnt[eng])
        self._commit(tok, reads, writes)
        self.q[eng].append((fn, waits, tok))

    def dma(self, eng, out_ap, in_ap, reads=(), writes=(), semres=None):
        sr = semres if semres is not None else writes[0]
        if sr.dma_sem is None:
            sr.dma_sem = "dma%d" % len(self.dma_res)
            self.dma_res.append(sr)
        waits = self._collect(eng, reads, writes)
        sr.dma_count += 16
        tok = (sr.dma_sem, sr.dma_count)
        self._commit(tok, reads, writes)

        def fn(e, out_ap=out_ap, in_ap=in_ap):
            return e.dma_start(out=out_ap, in_=in_ap)
        self.q[eng].append((fn, waits, tok))

    def custom(self, eng, fn, reads, writes, semres):
        if semres.dma_sem is None:
            semres.dma_sem = "cc%d" % len(self.dma_res)
            self.dma_res.append(semres)
        waits = self._collect(eng, reads, writes)
        semres.dma_count += 1
        tok = (semres.dma_sem, semres.dma_count)
        self._commit(tok, reads, writes)
        self.q[eng].append((fn, waits, tok))

    def barrier(self):
        toks = [(e, self.cnt[e]) for e in ENGINES if self.cnt[e] > 0]
        toks += [(r.dma_sem, r.dma_count) for r in self.dma_res if r.dma_count > 0]
        for e in ENGINES:
            waits = []
            for k, v in toks:
                if k == e:
                    if e == "pe":
                        continue
                if self.seen[e].get(k, 0) >= v:
                    continue
                self.seen[e][k] = v
                waits.append((k, v))
            self.q[e].append((None, waits, None))

    def engine_barrier(self, e):
        toks = [(k, self.cnt[k]) for k in ENGINES if self.cnt[k] > 0]
        toks += [(r.dma_sem, r.dma_count) for r in self.dma_res if r.dma_count > 0]
        waits = []
        for k, v in toks:
            if k == e and e == "pe":
                continue
            if self.seen[e].get(k, 0) >= v:
                continue
            self.seen[e][k] = v
            waits.append((k, v))
        self.q[e].append((None, waits, None))

    def emit(self):
        nc = self.nc
        with contextlib.ExitStack() as st:
            sems = {}
            for e in ENGINES:
                sems[e] = st.enter_context(nc.semaphore("s_" + e))
            for r in self.dma_res:
                sems[r.dma_sem] = st.enter_context(nc.semaphore("s_" + r.dma_sem))
            block = st.enter_context(nc.Block())

            def run(eng_name, eng):
                for fn, waits, tok in self.q[eng_name]:
                    for k, v in waits:
                        eng.wait_ge(sems[k], v)
                    if fn is None:
                        continue
                    ins = fn(eng)
                    if tok is not None:
                        k, v = tok
                        ins.then_inc(sems[k], 16 if k.startswith("dma") else 1)

            @block.tensor
            def _(e):
                run("pe", e)

            @block.scalar
            def _(e):
                run("act", e)

            @block.vector
            def _(e):
                run("dve", e)

            @block.gpsimd
            def _(e):
                run("pool", e)

            @block.sync
            def _(e):
                run("sp", e)


class Arena:
    def __init__(self, tensor, nbytes):
        self.t = tensor
        self.nbytes = nbytes
        self.off = 0

    def alloc(self, cols, dtype):
        esz = 4 if dtype == F32 else 2
        nb = cols * esz
        nb4 = (nb + 63) // 64 * 64
        assert self.off + nb4 <= self.nbytes, ("arena overflow", self.off, nb4, self.nbytes)
        a = self.t[:, self.off // 4:(self.off + nb4) // 4]
        self.off += nb4
        if dtype != F32:
            a = a.bitcast(dtype)
        return a[:, 0:cols]

    def mark(self):
        return self.off

    def reset(self, m):
        self.off = m


W_SHAPES = {
    "w_in": [2, 1024, 4352], "w_up_a": [2, 512, 1024], "w_up_b": [2, 512, 1024],
    "w_o": [2, 1024, 1024], "w_ff1": [2, 1024, 4096], "w_ff2": [2, 4096, 1024],
    "w_pe": [2, 256, 1024], "w_pg": [2, 1024, 1024],
}


def build(steps, fused=False, n_cores=8, debug=False):
    nc = bass.Bass("TRN2", target_bir_lowering=False)
    P = Prog(nc)
    kinds = [s[0] for s in steps]
    has = lambda k, l: (k, l) in steps
    layers = sorted({s[1] for s in steps if len(s) > 1})

    def din(name, shape, dt=F32):
        return nc.dram_tensor(name, shape, dt, kind="ExternalInput").ap()

    def dout(name, shape, dt=F32):
        return nc.dram_tensor(name, shape, dt, kind="ExternalOutput").ap()

    def dint(name, shape, dt=F32):
        return nc.dram_tensor(name, shape, dt, kind="Internal").ap()

    consts_d = din("consts", [128, 4 * 128])
    gcols_d = din("gcols", [128, 56])
    sink_d = din("sinkcol", [128, 16])
    need_x_in = True
    xin_d = din("xin", [1024, NT])
    final = ("F",) in steps
    out_d = dout("outT", [1024, NT]) if final else None
    xout_d = None
    if not final and any(k == "C" for k in kinds):
        xout_d = dout("xout", [1024, NT])
    wd = {}
    needA = any(k == "A" for k in kinds)
    needC = any(k == "C" for k in kinds)
    needB = any(k == "B" for k in kinds)
    if needA:
        wd["w_in"] = din("w_in", W_SHAPES["w_in"])
    if needC:
        for k in W_SHAPES:
            if k not in wd:
                wd[k] = din(k, W_SHAPES[k])
        pT_d = din("pT", [2, 256, NT])
    if needB:
        biasg_d = din("biasg", [128, 8 * 384])
        maskc_d = din("maskc", [128, 384])
        sbmask_d = din("sbmask", [128, 256])
    q_d, kx_d, vx_d, kxall_d, vxall_d = {}, {}, {}, {}, {}
    for l in layers:
        a_here, b_here = has("A", l), has("B", l)
        if fused:
            q_d[l] = dint("q%d" % l, [1024, NT], BF16)
            kx_d[l] = [dint("kx%d_%d" % (l, pt), [384, NT], BF16) for pt in range(2)]
            vx_d[l] = [dint("vx%d_%d" % (l, pt), [NT // 2, 640], BF16) for pt in range(2)]
            kxall_d[l] = [dint("kxall%d_%d" % (l, pt), [2 * 384, NT], BF16) for pt in range(2)]
            vxall_d[l] = [dint("vxall%d_%d" % (l, pt), [NT, 640], BF16) for pt in range(2)]
        else:
            if a_here:
                q_d[l] = (dint if b_here else dout)("q%d" % l, [1024, NT], BF16)
                kx_d[l] = [dout("kx%d_%d" % (l, pt), [384, NT], BF16) for pt in range(2)]
                vx_d[l] = [dout("vx%d_%d" % (l, pt), [NT // 2, 640], BF16) for pt in range(2)]
            if b_here:
                if not a_here:
                    q_d[l] = din("q%d" % l, [1024, NT], BF16)
                kxall_d[l] = [din("kxall%d_%d" % (l, pt), [2 * 384, NT], BF16) for pt in range(2)]
                vxall_d[l] = [din("vxall%d_%d" % (l, pt), [NT, 640], BF16) for pt in range(2)]
    ot_d = (dout if debug else dint)("ot", [1024, NT], BF16) if needB or needC else None

    st = contextlib.ExitStack()
    with st:
        xT = st.enter_context(nc.sbuf_tensor("xT_sb", [128, 8, NT], F32))
        hT = st.enter_context(nc.sbuf_tensor("hT", [128, 8, NT], BF16))
        cst = st.enter_context(nc.sbuf_tensor("cst", [128, 4 * 128], BF16))
        cstf = st.enter_context(nc.sbuf_tensor("cstf", [128, 4 * 128], F32))
        gcols = st.enter_context(nc.sbuf_tensor("gcols_sb", [128, 56], F32))
        sinkc = st.enter_context(nc.sbuf_tensor("sinkc_sb", [128, 16], F32))
        onec = st.enter_context(nc.sbuf_tensor("onec", [128, 2], F32))
        ARENA_BYTES = 90 * 1024
        scr = st.enter_context(nc.sbuf_tensor("scr", [128, ARENA_BYTES // 4], F32))
        ar = Arena(scr, ARENA_BYTES)
        psall_t = st.enter_context(nc.psum_tensor("psall", [128, 7 * 512], F32))
        psall = psall_t[:, :]
        psb = [psall[:, i * 512:(i + 1) * 512] for i in range(7)]
        pst_t = st.enter_context(nc.psum_tensor("pst", [128, 1024], BF16))
        pst = pst_t[:, :]
        PS = [Res("ps%d" % i) for i in range(7)]
        PST = Res("pst")

        ident = cst[:, 0:128]
        negU = cst[:, 128:256]
        onesB = cst[:, 256:384]
        onesM = cst[:, 384:512]

        X = [[Res("x%d_%d" % (c, g)) for g in range(NG)] for c in range(8)]
        H = [[Res("h%d_%d" % (c, g)) for g in range(NG)] for c in range(8)]
        R_cst, R_cstf, R_g, R_sink, R_one = Res("cst"), Res("cstf"), Res("gcols"), Res("sink"), Res("one")
        R_dram = {}

        def rd(name):
            if name not in R_dram:
                R_dram[name] = Res("d_" + name)
            return R_dram[name]

        def gs(g):
            return slice(g * 512, (g + 1) * 512)

        P.dma("sp", cstf[:], consts_d, writes=[R_cstf])
        P.dma("sp", gcols[:], gcols_d, writes=[R_g])
        P.dma("sp", sinkc[:], sink_d, writes=[R_sink])
        P.op("pool", lambda e: e.tensor_copy(out=cst[:], in_=cstf[:]), reads=[R_cstf], writes=[R_cst])
        P.op("pool", lambda e: e.memset(onec[:, 0:1], 1.0), writes=[R_one])
        P.op("pool", lambda e: e.memset(onec[:, 1:2], EPS), writes=[R_one])
        xin_v = xin_d.rearrange("(c p) t -> p c t", p=128)
        for c in range(8):
            P.dma("sp", xT[:, c, :], xin_v[:, c, :], writes=[X[c][g] for g in range(NG)])

        stg_state = {"i": 0}

        def make_wloader(nslots_stg=2, cast_eng="pool"):
            stgs = [ar.alloc(1024, F32) for _ in range(nslots_stg)]
            rs = [Res("stg%d" % i) for i in range(nslots_stg)]

            def wload1(dst3, dst_res, src2, kc, ncols):
                i = stg_state["i"] % nslots_stg
                stg_state["i"] += 1
                sv = stgs[i][:, 0:kc * ncols].rearrange("p (c n) -> p c n", c=kc)
                P.dma("sp", sv, src2.rearrange("(c p) n -> p c n", p=128), writes=[rs[i]])
                P.op(cast_eng, lambda e: e.tensor_copy(out=dst3, in_=sv), reads=[rs[i]], writes=[dst_res])

            def wload(dst3, dst_res, src2, kc, ncols):
                kcc = max(1, 1024 // ncols)
                for c0 in range(0, kc, kcc):
                    c1 = min(kc, c0 + kcc)
                    wload1(dst3[:, c0:c1, :], dst_res, src2[c0 * 128:c1 * 128, :], c1 - c0, ncols)
            return wload

        def make_T4():
            return [ar.alloc(512, F32) for _ in range(4)], [Res("T4_%d" % i) for i in range(4)]

        def rmsnorm(gidx, T4, R_T4, out_final=None):
            m = ar.mark()
            sq = [T4[2].bitcast(BF16)[:, 0:512], T4[3].bitcast(BF16)[:, 0:512]]
            Rsq = [R_T4[2], R_T4[3]]
            rstd = [T4[0], T4[1]]
            Rr = [R_T4[0], R_T4[1]]
            if out_final is not None:
                ofl = [ar.alloc(512, F32) for _ in range(2)]
                Ro = [Res("of0"), Res("of1")]
            k = 0
            for g in range(NG):
                bank = g % 2
                for c in range(8):
                    s = k % 2
                    k += 1
                    P.op("act", lambda e, s=s, c=c, g=g: e.activation(out=sq[s], in_=xT[:, c, gs(g)], func=AF.Square),
                         reads=[X[c][g]], writes=[Rsq[s]])
                    P.op("pe", lambda e, s=s, c=c, bank=bank: e.matmul(psb[bank][:, :], lhsT=onesM, rhs=sq[s], start=(c == 0), stop=(c == 7)),
                         reads=[Rsq[s], R_cst], writes=[PS[bank]])
                P.op("act", lambda e, bank=bank: e.activation(out=rstd[bank], in_=psb[bank][:, :], func=AF.Ln, bias=onec[:, 1:2], scale=1.0),
                     reads=[PS[bank], R_one], writes=[Rr[bank]])
                P.op("act", lambda e, bank=bank: e.activation(out=rstd[bank], in_=rstd[bank], func=AF.Exp, scale=-0.5),
                     reads=[Rr[bank]], writes=[Rr[bank]])
                for c in range(8):
                    gcol = gcols[:, gidx * 8 + c: gidx * 8 + c + 1]
                    if out_final is None:
                        P.op("dve", lambda e, c=c, g=g, bank=bank, gcol=gcol: e.scalar_tensor_tensor(
                            out=hT[:, c, gs(g)], in0=xT[:, c, gs(g)], scalar=gcol, in1=rstd[bank], op0=ALU.mult, op1=ALU.mult),
                            reads=[X[c][g], Rr[bank], R_g], writes=[H[c][g]])
                    else:
                        s = (g * 8 + c) % 2
                        P.op("dve", lambda e, c=c, g=g, bank=bank, gcol=gcol, s=s: e.scalar_tensor_tensor(
                            out=ofl[s], in0=xT[:, c, gs(g)], scalar=gcol, in1=rstd[bank], op0=ALU.mult, op1=ALU.mult),
                            reads=[X[c][g], Rr[bank], R_g], writes=[Ro[s]])
                        P.dma("act", out_final[c * 128:(c + 1) * 128, gs(g)], ofl[s], reads=[Ro[s]], writes=[rd("out")], semres=Ro[s])
            ar.reset(m)

        def phase_A(l):
            m = ar.mark()
            R_dummyA = Res("dummyA")
            T4, R_T4 = make_T4()
            rmsnorm(3 * l + 0, T4, R_T4)
            wload = make_wloader(4, cast_eng="dve")
            w_in = wd["w_in"]
            NW = 5
            wp = [ar.alloc(8 * 256, BF16).rearrange("p (c n) -> p c n", c=8) for _ in range(NW)]
            Rw = [Res("wpA%d" % i) for i in range(NW)]
            ot = [ar.alloc(512, BF16) for _ in range(4)]
            Rot = [Res("otA%d" % i) for i in range(4)]
            qd, kd, vd = q_d[l], kx_d[l], vx_d[l]
            wv = ar.alloc(8 * 640, BF16).rearrange("p (c n) -> p c n", c=8)
            Rwv = Res("wv")
            wload(wv[:, :, 0:256], Rwv, w_in[l, :, 1024:1280], 8, 256)
            wload(wv[:, :, 256:512], Rwv, w_in[l, :, 1280:1536], 8, 256)
            wload(wv[:, :, 512:640], Rwv, w_in[l, :, 2176:2304], 8, 128)
            vt = [ar.alloc(640, BF16) for _ in range(2)]
            Rvt = [Res("vt0"), Res("vt1")]
            for tb in range(16):
                g = tb // 4
                b0, b1 = (0, 1) if tb % 2 == 0 else (2, 3)
                for c in range(8):
                    lhs = hT[:, c, tb * 128:(tb + 1) * 128]
                    P.op("pe", lambda e, c=c, lhs=lhs, b0=b0: e.matmul(psb[b0][:, :], lhsT=lhs, rhs=wv[:, c, 0:512], start=(c == 0), stop=(c == 7)),
                         reads=[Rwv, H[c][g]], writes=[PS[b0]])
                    P.op("pe", lambda e, c=c, lhs=lhs, b1=b1: e.matmul(psb[b1][:, 0:128], lhsT=lhs, rhs=wv[:, c, 512:640], start=(c == 0), stop=(c == 7)),
                         reads=[Rwv, H[c][g]], writes=[PS[b1]])
                for _ in range(KEEP_WARM_A):
                    P.op("pe", lambda e: e.matmul(psb[6][:, 0:512], lhsT=onesB, rhs=cst[:, 0:512], start=True, stop=True),
                         reads=[R_cst], writes=[R_dummyA])
                s = tb % 2
                P.op("act", lambda e, s=s, b0=b0: e.activation(out=vt[s][:, 0:512], in_=psb[b0][:, :], func=AF.Copy),
                     reads=[PS[b0]], writes=[Rvt[s]])
                P.op("act", lambda e, s=s, b1=b1: e.activation(out=vt[s][:, 512:640], in_=psb[b1][:, 0:128], func=AF.Copy),
                     reads=[PS[b1]], writes=[Rvt[s]])
                vdd = vd[tb // 8]
                P.dma("act", vdd[(tb % 8) * 128:(tb % 8 + 1) * 128, :], vt[s], reads=[Rvt[s]], writes=[rd(vdd.tensor.name)], semres=Rvt[s])
            exchange_parts("v", l)
            panels = []
            for pn in range(2):
                panels.append((512 + pn * 256, kd, pn * 2, 1.0, False))
            panels.append((2048, kd, 4, 1.0, True))
            for pn in range(2):
                panels.append((pn * 256, qd, pn * 2, 0.125, False))
            for pn in range(2):
                panels.append((1536 + pn * 256, qd, 4 + pn * 2, 0.125, False))
            k = 0
            ev = 0
            def load_panel(q):
                (col0, dst, rb0, scale, dup) = panels[q]
                s = q % NW
                if not dup:
                    wload(wp[s], Rw[s], w_in[l, :, col0:col0 + 256], 8, 256)
                else:
                    for kv in range(2):
                        for dd in range(2):
                            wload(wp[s][:, :, (2 * kv + dd) * 64:(2 * kv + dd + 1) * 64], Rw[s],
                                  w_in[l, :, col0 + kv * 64: col0 + (kv + 1) * 64], 8, 64)
            pl = {"n": 0}
            for pidx, (col0, dst, rb0, scale, dup) in enumerate(panels):
                if pidx == 3:
                    exchange_parts("k", l)
                s = k % NW
                k += 1
                while pl["n"] < min(len(panels), pidx + 4):
                    load_panel(pl["n"])
                    pl["n"] += 1
                for ocb in range(2):
                    for g in range(NG):
                        bank = 2 + (ev % 4)
                        for c in range(8):
                            P.op("pe", lambda e, s=s, c=c, g=g, ocb=ocb, bank=bank: e.matmul(
                                psb[bank][:, :], lhsT=wp[s][:, c, ocb * 128:(ocb + 1) * 128], rhs=hT[:, c, gs(g)],
                                start=(c == 0), stop=(c == 7)),
                                reads=[Rw[s], H[c][g]], writes=[PS[bank]])
                        for _ in range(KEEP_WARM_A):
                            P.op("pe", lambda e: e.matmul(psb[6][:, 0:512], lhsT=onesB, rhs=cst[:, 0:512], start=True, stop=True),
                                 reads=[R_cst], writes=[R_dummyA])
                        o = ev % 4
                        if ev % 2 == 0:
                            P.op("act", lambda e, o=o, bank=bank, scale=scale: e.activation(out=ot[o], in_=psb[bank][:, :], func=AF.Copy, scale=scale),
                                 reads=[PS[bank]], writes=[Rot[o]])
                        else:
                            P.op("dve", lambda e, o=o, bank=bank, scale=scale: e.tensor_scalar(out=ot[o], in0=psb[bank][:, :], scalar1=scale, scalar2=None, op0=ALU.mult),
                                 reads=[PS[bank]], writes=[Rot[o]])
                        rb = rb0 + ocb
                        if dst is kd:
                            dd = kd[rb // 3]
                            rb = rb % 3
                        else:
                            dd = dst
                        P.dma("act", dd[rb * 128:(rb + 1) * 128, gs(g)], ot[o], reads=[Rot[o]], writes=[rd(dd.tensor.name)], semres=Rot[o])
                        ev += 1
            ar.reset(m)

        def exchange_parts(kind, l):
            if not fused:
                return
            pairs = list(zip(kx_d[l], kxall_d[l])) if kind == "k" else list(zip(vx_d[l], vxall_d[l]))
            groups = [[2 * i, 2 * i + 1] for i in range(n_cores // 2)]
            P.engine_barrier("pool")
            for src, dst in pairs:
                P.custom("pool", lambda e, src=src, dst=dst: e.collective_compute(
                    "AllGather", ALU.bypass, replica_groups=groups, ins=[src.opt()], outs=[dst.opt()]),
                    reads=[rd(src.tensor.name)], writes=[rd(dst.tensor.name)], semres=rd(dst.tensor.name))

        def phase_B(l):
            m = ar.mark()
            qd, kall, vall = q_d[l], kxall_d[l], vxall_d[l]
            Rq = rd(qd.tensor.name)
            Rk = [rd(t.tensor.name) for t in kall]
            Rv = [rd(t.tensor.name) for t in vall]

            def krows(r, blk):
                t = kall[blk // 3]
                o = r * 384 + (blk % 3) * 128
                return t[o:o + 128, :]
            Rot_d = rd("ot")
            sbm_f = ar.alloc(256, F32)
            sbm = ar.alloc(256, BF16)
            R_sbmf, R_sbm = Res("sbmf"), Res("sbm")
            P.dma("sp", sbm_f, sbmask_d, writes=[R_sbmf])
            P.op("pool", lambda e: e.tensor_copy(out=sbm, in_=sbm_f), reads=[R_sbmf], writes=[R_sbm])
            maskLo, maskHi = sbm[:, 0:128], sbm[:, 128:256]
            bm = ar.alloc(8 * 384, F32)
            mk = ar.alloc(384, F32)
            R_bm, R_mk = Res("bm"), Res("mk")
            P.dma("sp", bm, biasg_d, writes=[R_bm])
            P.dma("sp", mk, maskc_d, writes=[R_mk])
            bm3 = bm.rearrange("p (h n) -> p h n", h=8)
            for h in range(8):
                P.op("pool", lambda e, h=h: e.tensor_tensor(out=bm3[:, h, :], in0=bm3[:, h, :], in1=mk, op=ALU.add),
                     reads=[R_mk, R_bm], writes=[R_bm])
            KT = [ar.alloc(4096, BF16) for _ in range(2)]
            VV = [ar.alloc(32 * 128, BF16).rearrange("p (s n) -> p s n", s=32) for _ in range(2)]
            QQ = [ar.alloc(NT, BF16) for _ in range(2)]
            RK, RV, RQ = [Res("KT0"), Res("KT1")], [Res("VV0"), Res("VV1")], [Res("QQ0"), Res("QQ1")]
            osb = [ar.alloc(512, BF16) for _ in range(2)]
            Rosb = [Res("osb0"), Res("osb1")]
            vall_v = [t.rearrange("(s p) n -> p s n", p=128) for t in vall]

            m2 = ar.mark()
            E_t = [ar.alloc(512, F32) for _ in range(2)]
            SP_t = [ar.alloc(512, BF16) for _ in range(3)]
            ARG_t = [ar.alloc(512, F32) for _ in range(2)]
            AT_t = [ar.alloc(512, BF16) for _ in range(3)]
            RB_t = [[ar.alloc(512, F32) for _ in range(2)] for _ in range(2)]
            R_E = [Res("E0"), Res("E1")]
            R_SP = [Res("SP%d" % i) for i in range(3)]
            R_ARG = [Res("ARG0"), Res("ARG1")]
            R_AT = [Res("AT%d" % i) for i in range(3)]
            R_RB = [[Res("RB%d%d" % (h, q)) for q in range(2)] for h in range(2)]
            ZB = [0, 1, 2]
            RPB = [3, 4]
            OB = [5, 6]

            units = []
            ocount = 0
            def sb_loads(pr):
                s = pr % 2
                P.dma("sp", QQ[s], qd[pr * 128:(pr + 1) * 128, :], reads=[Rq], writes=[RQ[s]])
                for r in range(2):
                    P.dma("sp", KT[s][:, r * 2048:(r + 1) * 2048], krows(r, pr), reads=Rk, writes=[RK[s]])
                for pt in range(2):
                    for r in range(2):
                        P.dma("sp", VV[s][:, r * 16 + pt * 8: r * 16 + pt * 8 + 8, :], vall_v[pt][:, r * 8:(r + 1) * 8, pr * 128:(pr + 1) * 128],
                              reads=Rv, writes=[RV[s]])

            first_idx = {}
            for pr in range(4):
                s = pr % 2
                first_idx[pr] = len(units)
                for mg in range(4):
                    ob = OB[ocount % 2]
                    osel = ocount % 2
                    ocount += 1
                    first = True
                    nk = 8 * mg + 8
                    for kb in range(nk - 1, -1, -1):
                        c0 = max(0, (kb - 8 * mg) // 2) if kb >= 8 * mg else 0
                        c0 = 0
                        while 2 * (4 * mg + c0) + 1 < kb:
                            c0 += 1
                        mask = None
                        for c in range(c0, 4):
                            lo = 2 * (4 * mg + c)
                            if kb == lo + 1:
                                mask = (c, maskHi)
                            elif kb == lo:
                                mask = (c, maskLo)
                        for h in range(2):
                            units.append(dict(pr=pr, s=s, mg=mg, kb=kb, c0=c0, mask=mask, h=h, ob=ob, osel=osel,
                                              first=(kb == nk - 1), last=(kb == 0)))

            def kcols(kb):
                r, j = kb % 2, kb // 2
                return slice(r * 2048 + j * 128, r * 2048 + (j + 1) * 128)

            def vslot(kb):
                r, j = kb % 2, kb // 2
                return r * 16 + j

            def s1(i, u):
                h, s, c0 = u["h"], u["s"], u["c0"]
                hp = slice(64 * h, 64 * h + 64)
                zb = ZB[i % 3]
                a0 = c0 * 128
                qcols = slice(u["mg"] * 512 + a0, (u["mg"] + 1) * 512)
                if u["first"]:
                    rb = RB_t[h][u["mg"] % 2]
                    P.op("pool", lambda e, rb=rb: e.memset(rb, 0.0), writes=[R_RB[h][u["mg"] % 2]])
                    if h == 0:
                        P.op("dve", lambda e, ob=u["ob"]: e.memset(psb[ob][:, :], 0.0), writes=[PS[u["ob"]]])
                has_mask = u["mask"] is not None
                P.op("pe", lambda e: e.matmul(psb[zb][:, a0:512], lhsT=KT[s][hp, kcols(u["kb"])], rhs=QQ[s][hp, qcols],
                                              start=True, stop=(not has_mask)),
                     reads=[RK[s], RQ[s]], writes=[PS[zb]])
                if has_mask:
                    c, mk_ap = u["mask"]
                    P.op("pe", lambda e: e.matmul(psb[zb][:, c * 128:(c + 1) * 128], lhsT=ident, rhs=mk_ap, start=False, stop=True),
                         reads=[R_cst, R_sbm], writes=[PS[zb]])

            def s1_act(i, u):
                c0 = u["c0"]
                zb = ZB[i % 3]
                a0 = c0 * 128
                et, spt = E_t[i % 2], SP_t[i % 3]
                P.op("act", lambda e: e.activation(out=et[:, a0:512], in_=psb[zb][:, a0:512], func=AF.Exp),
                     reads=[PS[zb]], writes=[R_E[i % 2]])

            def s1_act2(i, u):
                c0 = u["c0"]
                a0 = c0 * 128
                et, spt = E_t[i % 2], SP_t[i % 3]
                P.op("act", lambda e: e.activation(out=spt[:, a0:512], in_=et[:, a0:512], func=AF.Ln, bias=1.0, scale=1.0),
                     reads=[R_E[i % 2], R_one], writes=[R_SP[i % 3]])

            def s2(i, u):
                h, c0 = u["h"], u["c0"]
                zb, rpb = ZB[i % 3], RPB[i % 2]
                a0 = c0 * 128
                spt = SP_t[i % 3]
                rb, Rrb = RB_t[h][u["mg"] % 2], R_RB[h][u["mg"] % 2]
                P.op("pe", lambda e: e.matmul(psb[zb][:, a0:512], lhsT=negU, rhs=spt[:, a0:512], start=False, stop=True, skip_group_check=True),
                     reads=[R_SP[i % 3], R_cst], writes=[PS[zb]])
                P.op("pe", lambda e: e.matmul(psb[rpb][:, a0:512], lhsT=onesB, rhs=spt[:, a0:512], start=True, stop=True),
                     reads=[R_SP[i % 3], R_cst], writes=[PS[rpb]])

            def s2_dve(i, u):
                h, c0 = u["h"], u["c0"]
                zb, rpb = ZB[i % 3], RPB[i % 2]
                a0 = c0 * 128
                rb, Rrb = RB_t[h][u["mg"] % 2], R_RB[h][u["mg"] % 2]
                argt = ARG_t[i % 2]
                P.op("dve", lambda e: e.tensor_tensor(out=argt[:, a0:512], in0=psb[zb][:, a0:512], in1=rb[:, a0:512], op=ALU.subtract),
                     reads=[PS[zb], Rrb], writes=[R_ARG[i % 2]])
                if not u["last"]:
                    P.op("dve", lambda e: e.tensor_tensor(out=rb[:, a0:512], in0=rb[:, a0:512], in1=psb[rpb][:, a0:512], op=ALU.add),
                         reads=[PS[rpb], Rrb], writes=[Rrb])

            def s3(i, u):
                h, s, c0 = u["h"], u["s"], u["c0"]
                a0 = c0 * 128
                argt, att = ARG_t[i % 2], AT_t[i % 3]
                ob = u["ob"]
                P.op("act", lambda e: e.activation(out=att[:, a0:512], in_=argt[:, a0:512], func=AF.Exp),
                     reads=[R_ARG[i % 2]], writes=[R_AT[i % 3]])

            def s3_pe(i, u):
                h, s, c0 = u["h"], u["s"], u["c0"]
                a0 = c0 * 128
                att = AT_t[i % 3]
                ob = u["ob"]
                vs = vslot(u["kb"])
                kw = dict(tile_position=(0, 64)) if h == 1 else {}
                P.op("pe", lambda e: e.matmul(psb[ob][64 * h:64 * h + 64, a0:512], lhsT=VV[s][:, vs, 64 * h:64 * h + 64],
                                              rhs=att[:, a0:512], start=False, stop=u["last"], skip_group_check=True, **kw),
                     reads=[R_AT[i % 3], RV[s]], writes=[PS[ob]])
                if u["last"] and h == 1:
                    osel = u["osel"]
                    P.op("act", lambda e: e.activation(out=osb[osel], in_=psb[ob][:, :], func=AF.Copy),
                         reads=[PS[ob]], writes=[Rosb[osel]])
                    P.dma("act", ot_d[u["pr"] * 128:(u["pr"] + 1) * 128, gs(u["mg"])], osb[osel], reads=[Rosb[osel]], writes=[Rot_d], semres=Rosb[osel])

            n = len(units)
            sb_loads(0)
            sb_loads(1)
            load_at = {first_idx[pr] + 6: pr + 1 for pr in range(1, 3)}
            pstF = pst.bitcast(F32)
            R_dummy = Res("dummy")

            def keep_warm():
                P.op("pe", lambda e: e.matmul(pstF[:, 0:512], lhsT=onesB, rhs=cst[:, 0:512], start=True, stop=True),
                     reads=[R_cst], writes=[R_dummy])

            s1(0, units[0])
            for i in range(n + 3):
                if i in load_at:
                    sb_loads(load_at[i])
                if 0 <= i - 1 < n:
                    s2(i - 1, units[i - 1])
                if i + 1 < n:
                    s1(i + 1, units[i + 1])
                if i < n:
                    for _ in range(KEEP_WARM + (1 if units[i]["c0"] >= 1 else 0)):
                        keep_warm()
                if 0 <= i - 3 < n:
                    s3_pe(i - 3, units[i - 3])
                if i < n:
                    s1_act(i, units[i])
                if 0 <= i - 1 < n:
                    s2_dve(i - 1, units[i - 1])
                if 0 <= i - 2 < n:
                    s3(i - 2, units[i - 2])
                if i < n:
                    s1_act2(i, units[i])

            P.barrier()
            ar.reset(m2)
            bhi = ar.alloc(8 * 384, BF16)
            blo = ar.alloc(8 * 384, BF16)
            R_bhl = Res("bhl")
            P.op("dve", lambda e: e.tensor_copy(out=bhi, in_=bm), reads=[R_bm], writes=[R_bhl])
            P.op("dve", lambda e: e.tensor_tensor(out=blo, in0=bm, in1=bhi, op=ALU.subtract), reads=[R_bm, R_bhl], writes=[R_bhl])
            bhi3 = bhi.rearrange("p (h n) -> p h n", h=8)
            blo3 = blo.rearrange("p (h n) -> p h n", h=8)
            negsink = ar.alloc(8, F32)
            R_ns = Res("negsink")
            P.op("dve", lambda e: e.tensor_scalar(out=negsink, in0=sinkc[:, l * 8:(l + 1) * 8], scalar1=-1.0, scalar2=None, op0=ALU.mult),
                 reads=[R_sink], writes=[R_ns])
            NB = 3
            Pf = [ar.alloc(768, F32).rearrange("p (h n) -> p h n", h=2) for _ in range(NB)]
            Pn = [ar.alloc(768, BF16).rearrange("p (h n) -> p h n", h=2) for _ in range(NB)]
            PTs = [ar.alloc(768, BF16) for _ in range(NB)]
            sm = [ar.alloc(16, F32) for _ in range(NB)]
            R_P = [Res("P%d" % i) for i in range(NB)]
            R_Pn = [Res("Pn%d" % i) for i in range(NB)]
            R_PT = [Res("PT%d" % i) for i in range(NB)]
            R_smA = [Res("smA%d" % i) for i in range(NB)]
            R_smB = [Res("smB%d" % i) for i in range(NB)]
            R_smC = [Res("smC%d" % i) for i in range(NB)]
            R_smD = [Res("smD%d" % i) for i in range(NB)]
            SBK = [(0, 1), (2, 3)]
            PTP = [pst[:, 0:768], psb[4].bitcast(BF16)[:, 0:768]]
            R_PTP = [PST, PS[4]]
            OBW = [5, 6]

            def sw_kv_loads(kv):
                s = kv % 2
                for r in range(2):
                    P.dma("sp", KT[s][:, r * 2048:(r + 1) * 2048], krows(r, 4 + kv), reads=Rk, writes=[RK[s]])
                for pt in range(2):
                    for r in range(2):
                        P.dma("sp", VV[s][:, r * 16 + pt * 8: r * 16 + pt * 8 + 8, 0:64],
                              vall_v[pt][:, r * 8:(r + 1) * 8, 512 + kv * 64: 512 + (kv + 1) * 64], reads=Rv, writes=[RV[s]])

            def sw_q_load(qi):
                P.dma("sp", QQ[qi % 2], qd[(4 + qi) * 128:(5 + qi) * 128, :], reads=[Rq], writes=[RQ[qi % 2]])

            swu = []
            ocw = 0
            for kv in range(2):
                for pp in range(2):
                    qi = 2 * kv + pp
                    for jg in range(4):
                        ob = OBW[ocw % 2]
                        osel = ocw % 2
                        ocw += 1
                        for jj in range(4):
                            swu.append(dict(kv=kv, pp=pp, qi=qi, jg=jg, jj=jj, j=4 * jg + jj, ob=ob, osel=osel,
                                            firstq=(jg == 0 and jj == 0)))

            def wins_of(j):
                w = [(1, 0, j), (2, 1, j)]
                if j > 0:
                    w = [(0, 1, j - 1)] + w
                return w

            def w1(i, u):
                kv, qi, j = u["kv"], u["qi"], u["j"]
                s, qs, b = kv % 2, qi % 2, i % NB
                if u["firstq"] and qi + 1 < 4 and qi + 1 >= 2:
                    sw_q_load(qi + 1)
                wins = wins_of(j)
                w0 = wins[0][0] * 128
                head0 = 4 * kv + 2 * u["pp"]
                if u["jj"] == 0:
                    P.op("dve", lambda e: e.memset(psb[u["ob"]][:, :], 0.0), writes=[PS[u["ob"]]])
                smt = sm[b]
                for half in range(2):
                    hp = slice(64 * half, 64 * half + 64)
                    bank = SBK[i % 2][half]
                    head = head0 + half
                    for wi, (w, r, sl) in enumerate(wins):
                        P.op("pe", lambda e, w=w, r=r, sl=sl, hp=hp, bank=bank, wi=wi: e.matmul(
                            psb[bank][:, w * 128:(w + 1) * 128], lhsT=QQ[qs][hp, j * 128:(j + 1) * 128],
                            rhs=KT[s][hp, r * 2048 + sl * 128: r * 2048 + (sl + 1) * 128], start=(wi == 0),
                            stop=True, skip_group_check=(wi > 0)),
                            reads=[RQ[qs], RK[s]], writes=[PS[bank]])
                    if half == 0:
                        P.op("pe", lambda e, bank=bank, head=head: e.matmul(psb[bank][:, w0:384], lhsT=ident, rhs=bhi3[:, head, w0:384], start=False, stop=True,
                                                                          skip_group_check=True),
                             reads=[R_cst, R_bhl], writes=[PS[bank]])
                        P.op("pe", lambda e, bank=bank, head=head: e.matmul(psb[bank][:, w0:384], lhsT=ident, rhs=blo3[:, head, w0:384], start=False, stop=True,
                                                                          skip_group_check=True),
                             reads=[R_cst, R_bhl], writes=[PS[bank]])

            def w1_rest(i, u):
                kv, qi, j = u["kv"], u["qi"], u["j"]
                b = i % NB
                wins = wins_of(j)
                w0 = wins[0][0] * 128
                head0 = 4 * kv + 2 * u["pp"]
                smt = sm[b]
                bk0, bk1 = SBK[i % 2]
                P.op("dve", lambda e: e.tensor_tensor(out=Pf[b][:, 1, w0:384], in0=psb[bk1][:, w0:384], in1=bm3[:, head0 + 1, w0:384], op=ALU.add),
                     reads=[PS[bk1], R_bm], writes=[R_P[b]])
                P.op("dve", lambda e: e.reduce_max(out=smt[:, 0:1], in_=psb[bk0][:, w0:384], axis=AX.X), reads=[PS[bk0]], writes=[R_smA[b]])
                P.op("dve", lambda e: e.reduce_max(out=smt[:, 1:2], in_=Pf[b][:, 1, w0:384], axis=AX.X), reads=[R_P[b]], writes=[R_smA[b]])
                P.op("dve", lambda e: e.scalar_tensor_tensor(out=smt[:, 2:4], in0=smt[:, 0:2], scalar=-1.0, in1=negsink[:, head0:head0 + 2],
                                                             op0=ALU.mult, op1=ALU.min),
                     reads=[R_smA[b], R_ns], writes=[R_smA[b]])
                P.op("act", lambda e: e.activation(
                    out=Pf[b][:, 0, w0:384], in_=psb[bk0][:, w0:384], func=AF.Exp, bias=smt[:, 2:3], scale=1.0, accum_out=smt[:, 4:5]),
                    reads=[PS[bk0], R_smA[b]], writes=[R_P[b], R_smB[b]])
                P.op("act", lambda e: e.activation(
                    out=Pf[b][:, 1, w0:384], in_=Pf[b][:, 1, w0:384], func=AF.Exp, bias=smt[:, 3:4], scale=1.0, accum_out=smt[:, 5:6]),
                    reads=[R_P[b], R_smA[b]], writes=[R_P[b], R_smB[b]])
                P.op("dve", lambda e: e.tensor_tensor(out=smt[:, 6:8], in0=sinkc[:, l * 8 + head0: l * 8 + head0 + 2], in1=smt[:, 2:4], op=ALU.add),
                     reads=[R_smA[b], R_sink], writes=[R_smC[b]])
                P.op("act", lambda e: e.activation(out=smt[:, 8:10], in_=smt[:, 6:8], func=AF.Exp), reads=[R_smC[b]], writes=[R_smC[b]])

            def w2(i, u):
                j = u["j"]
                b = i % NB
                wins = wins_of(j)
                w0 = wins[0][0] * 128
                smt = sm[b]
                ptp, Rptp = PTP[i % 2], R_PTP[i % 2]
                P.op("dve", lambda e: e.tensor_tensor(out=smt[:, 10:12], in0=smt[:, 4:6], in1=smt[:, 8:10], op=ALU.add), reads=[R_smB[b], R_smC[b]], writes=[R_smD[b]])
                P.op("dve", lambda e: e.reciprocal(out=smt[:, 12:14], in_=smt[:, 10:12]), reads=[R_smD[b]], writes=[R_smD[b]])
                for half in range(2):
                    P.op("dve", lambda e, half=half: e.tensor_scalar(out=Pn[b][:, half, w0:384], in0=Pf[b][:, half, w0:384],
                                                                     scalar1=smt[:, 12 + half:13 + half], scalar2=None, op0=ALU.mult),
                         reads=[R_P[b], R_smD[b]], writes=[R_Pn[b]])

            def w2_pe(i, u):
                j = u["j"]
                b = i % NB
                wins = wins_of(j)
                ptp, Rptp = PTP[i % 2], R_PTP[i % 2]
                for half in range(2):
                    for (w, r, sl) in wins:
                        P.op("pe", lambda e, half=half, w=w: e.transpose(out=ptp[:, half * 384 + w * 128: half * 384 + (w + 1) * 128],
                                                                         in_=Pn[b][:, half, w * 128:(w + 1) * 128], identity=ident),
                             reads=[R_Pn[b], R_cst], writes=[Rptp])

            def w2_act(i, u):
                j = u["j"]
                b = i % NB
                wins = wins_of(j)
                w0 = wins[0][0] * 128
                ptp, Rptp = PTP[i % 2], R_PTP[i % 2]
                src3 = ptp.rearrange("p (h n) -> p h n", h=2)[:, :, w0:384]
                dst3 = PTs[b].rearrange("p (h n) -> p h n", h=2)[:, :, w0:384]
                P.op("act", lambda e: e.activation(out=dst3, in_=src3, func=AF.Copy), reads=[Rptp], writes=[R_PT[b]])

            def w3(i, u):
                kv, j, jj, ob = u["kv"], u["j"], u["jj"], u["ob"]
                s, b = kv % 2, i % NB
                wins = wins_of(j)
                for half in range(2):
                    kw = dict(tile_position=(0, 64)) if half == 1 else {}
                    for wi, (w, r, sl) in enumerate(wins):
                        P.op("pe", lambda e, half=half, w=w, r=r, sl=sl, wi=wi, kw=kw: e.matmul(
                            psb[ob][64 * half:64 * half + 64, jj * 128:(jj + 1) * 128], lhsT=VV[s][:, r * 16 + sl, 0:64],
                            rhs=PTs[b][:, half * 384 + w * 128: half * 384 + (w + 1) * 128], start=False, stop=(wi == len(wins) - 1),
                            skip_group_check=True, **kw),
                            reads=[R_PT[b], RV[s]], writes=[PS[ob]])
                if jj == 3:
                    osel = u["osel"]
                    qrow = 4 + u["qi"]
                    P.op("act", lambda e: e.activation(out=osb[osel], in_=psb[ob][:, :], func=AF.Copy), reads=[PS[ob]], writes=[Rosb[osel]])
                    P.dma("act", ot_d[qrow * 128:(qrow + 1) * 128, gs(u["jg"])], osb[osel], reads=[Rosb[osel]], writes=[Rot_d], semres=Rosb[osel])

            R_dummySW = Res("dummySW")
            sw_kv_loads(0)
            sw_kv_loads(1)
            sw_q_load(0)
            sw_q_load(1)
            nsw = len(swu)
            w1(0, swu[0])
            for i in range(nsw + 3):
                if 0 <= i - 3 < nsw:
                    w3(i - 3, swu[i - 3])
                if i + 1 < nsw:
                    w1(i + 1, swu[i + 1])
                for dk in range(KEEP_WARM_SW):
                    tgt = pst.bitcast(F32)[:, 384:512] if dk % 2 == 0 else psb[4][:, 384:512]
                    P.op("pe", lambda e, tgt=tgt: e.matmul(tgt, lhsT=onesB, rhs=cst[:, 128:256], start=False, stop=True, skip_group_check=True),
                         reads=[R_cst], writes=[R_dummySW])
                if 0 <= i - 2 < nsw:
                    w2(i - 2, swu[i - 2])
                    w2_pe(i - 2, swu[i - 2])
                if i < nsw:
                    w1_rest(i, swu[i])
                if 0 <= i - 2 < nsw:
                    w2_act(i - 2, swu[i - 2])
            ar.reset(m)

        def phase_C(l):
            m = ar.mark()
            Rot_d = rd("ot")
            wload = make_wloader()
            big = ar.alloc(8 * NT, BF16).rearrange("p (c t) -> p c t", c=8)
            BG = [[Res("big%d_%d" % (c, g)) for g in range(NG)] for c in range(8)]
            NW = 6
            wp = [ar.alloc(8 * 256, BF16).rearrange("p (c n) -> p c n", c=8) for _ in range(NW)]
            Rw = [Res("wpC%d" % i) for i in range(NW)]
            T4, R_T4 = make_T4()
            w_in = wd["w_in"]
            reqs = []
            for pn in range(4):
                reqs += [(w_in[l, :, 2304 + pn * 256: 2304 + (pn + 1) * 256], 8), (w_in[l, :, 3328 + pn * 256: 3328 + (pn + 1) * 256], 8),
                         (wd["w_up_a"][l, :, pn * 256:(pn + 1) * 256], 4), (wd["w_up_b"][l, :, pn * 256:(pn + 1) * 256], 4)]
            for pn in range(4):
                reqs.append((wd["w_o"][l, :, pn * 256:(pn + 1) * 256], 8))
            for qf in range(4):
                for pn in range(4):
                    reqs.append((wd["w_ff1"][l, :, qf * 1024 + pn * 256: qf * 1024 + (pn + 1) * 256], 8))
                for pn in range(4):
                    reqs.append((wd["w_ff2"][l, qf * 1024:(qf + 1) * 1024, pn * 256:(pn + 1) * 256], 8))
            for pn in range(4):
                reqs += [(wd["w_pe"][l, :, pn * 256:(pn + 1) * 256], 2), (wd["w_pg"][l, :, pn * 256:(pn + 1) * 256], 8)]
            wst = {"issued": 0, "next": 0}

            def begin_step():
                upto = min(wst["next"] + NW, len(reqs))
                while wst["issued"] < upto:
                    r = wst["issued"]
                    src2, kc = reqs[r]
                    wload(wp[r % NW][:, 0:kc, :], Rw[r % NW], src2, kc, 256)
                    wst["issued"] += 1

            def getw(src2, kc):
                r = wst["next"]
                wst["next"] += 1
                assert r < wst["issued"] and reqs[r][1] == kc
                return wp[r % NW], Rw[r % NW]

            og = [ar.alloc(8 * 512, BF16).rearrange("p (k t) -> p k t", k=8) for _ in range(2)]
            Rog = [Res("og0"), Res("og1")]
            sg = [T4[0], T4[1]]
            m1 = [T4[2], T4[3]]
            Rsg = [R_T4[0], R_T4[1]]
            Rm1 = [R_T4[2], R_T4[3]]
            ot_v = ot_d.rearrange("(k p) t -> p k t", p=128)
            w_in = wd["w_in"]
            ogi = 0
            for pn in range(4):
                begin_step()
                wga, Rga = getw(w_in[l, :, 2304 + pn * 256: 2304 + (pn + 1) * 256], 8)
                wgb, Rgb = getw(w_in[l, :, 3328 + pn * 256: 3328 + (pn + 1) * 256], 8)
                wua, Rua = getw(wd["w_up_a"][l, :, pn * 256:(pn + 1) * 256], 4)
                wub, Rub = getw(wd["w_up_b"][l, :, pn * 256:(pn + 1) * 256], 4)
                for g in range(NG):
                    o = ogi % 2
                    ogi += 1
                    P.dma("sp", og[o], ot_v[:, :, gs(g)], reads=[Rot_d], writes=[Rog[o]])
                    for ocb in range(2):
                        oc = pn * 2 + ocb
                        cs = slice(ocb * 128, (ocb + 1) * 128)
                        for c in range(8):
                            P.op("pe", lambda e, c=c, g=g, cs=cs, wga=wga: e.matmul(psb[0][:, :], lhsT=wga[:, c, cs], rhs=hT[:, c, gs(g)], start=(c == 0), stop=(c == 7)),
                                 reads=[Rga, H[c][g]], writes=[PS[0]])
                        for c in range(4):
                            P.op("pe", lambda e, c=c, o=o, cs=cs, wua=wua: e.matmul(psb[1][:, :], lhsT=wua[:, c, cs], rhs=og[o][:, c, :], start=(c == 0), stop=(c == 3)),
                                 reads=[Rua, Rog[o]], writes=[PS[1]])
                        for c in range(8):
                            P.op("pe", lambda e, c=c, g=g, cs=cs, wgb=wgb: e.matmul(psb[2][:, :], lhsT=wgb[:, c, cs], rhs=hT[:, c, gs(g)], start=(c == 0), stop=(c == 7)),
                                 reads=[Rgb, H[c][g]], writes=[PS[2]])
                        for c in range(4):
                            P.op("pe", lambda e, c=c, o=o, cs=cs, wub=wub: e.matmul(psb[3][:, :], lhsT=wub[:, c, cs], rhs=og[o][:, 4 + c, :], start=(c == 0), stop=(c == 3)),
                                 reads=[Rub, Rog[o]], writes=[PS[3]])
                        P.op("act", lambda e: e.activation(out=sg[0], in_=psb[0][:, :], func=AF.Sigmoid), reads=[PS[0]], writes=[Rsg[0]])
                        P.op("dve", lambda e: e.tensor_tensor(out=m1[0], in0=sg[0], in1=psb[1][:, :], op=ALU.mult), reads=[Rsg[0], PS[1]], writes=[Rm1[0]])
                        P.op("act", lambda e: e.activation(out=sg[1], in_=psb[2][:, :], func=AF.Sigmoid), reads=[PS[2]], writes=[Rsg[1]])
                        P.op("dve", lambda e: e.tensor_tensor(out=m1[1], in0=sg[1], in1=psb[3][:, :], op=ALU.mult), reads=[Rsg[1], PS[3]], writes=[Rm1[1]])
                        P.op("dve", lambda e, oc=oc, g=g: e.tensor_tensor(out=big[:, oc, gs(g)], in0=m1[0], in1=m1[1], op=ALU.add),
                             reads=[Rm1[0], Rm1[1]], writes=[BG[oc][g]])
            ev = 0
            for pn in range(4):
                begin_step()
                wo, Ro = getw(wd["w_o"][l, :, pn * 256:(pn + 1) * 256], 8)
                for g in range(NG):
                    for ocb in range(2):
                        oc = pn * 2 + ocb
                        cs = slice(ocb * 128, (ocb + 1) * 128)
                        bank = 4 + ev % 3
                        ev += 1
                        for c in range(8):
                            P.op("pe", lambda e, c=c, g=g, cs=cs, wo=wo, bank=bank: e.matmul(psb[bank][:, :], lhsT=wo[:, c, cs], rhs=big[:, c, gs(g)], start=(c == 0), stop=(c == 7)),
                                 reads=[Ro, BG[c][g]], writes=[PS[bank]])
                        P.op("dve", lambda e, oc=oc, g=g, bank=bank: e.tensor_tensor(out=xT[:, oc, gs(g)], in0=xT[:, oc, gs(g)], in1=psb[bank][:, :], op=ALU.add),
                             reads=[PS[bank], X[oc][g]], writes=[X[oc][g]])
            rmsnorm(3 * l + 1, T4, R_T4)
            rl = [T4[0], T4[1]]
            Rrl = [R_T4[0], R_T4[1]]
            ev = 0
            for qf in range(4):
                for pn in range(4):
                    col0 = qf * 1024 + pn * 256
                    begin_step()
                    w1, R1 = getw(wd["w_ff1"][l, :, col0:col0 + 256], 8)
                    for g in range(NG):
                        for fb in range(2):
                            fc = pn * 2 + fb
                            cs = slice(fb * 128, (fb + 1) * 128)
                            bank = ev % 4
                            rs = ev % 2
                            ev += 1
                            for c in range(8):
                                P.op("pe", lambda e, c=c, g=g, cs=cs, w1=w1, bank=bank: e.matmul(psb[bank][:, :], lhsT=w1[:, c, cs], rhs=hT[:, c, gs(g)], start=(c == 0), stop=(c == 7)),
                                     reads=[R1, H[c][g]], writes=[PS[bank]])
                            P.op("act", lambda e, rs=rs, bank=bank: e.activation(out=rl[rs], in_=psb[bank][:, :], func=AF.Relu),
                                 reads=[PS[bank]], writes=[Rrl[rs]])
                            P.op("dve", lambda e, rs=rs, fc=fc, g=g: e.tensor_tensor(out=big[:, fc, gs(g)], in0=rl[rs], in1=rl[rs], op=ALU.mult),
                                 reads=[Rrl[rs]], writes=[BG[fc][g]])
                for pn in range(4):
                    begin_step()
                    w2, R2 = getw(wd["w_ff2"][l, qf * 1024:(qf + 1) * 1024, pn * 256:(pn + 1) * 256], 8)
                    for g in range(NG):
                        for ocb in range(2):
                            oc = pn * 2 + ocb
                            cs = slice(ocb * 128, (ocb + 1) * 128)
                            bank = 4 + ev % 3
                            ev += 1
                            for c in range(8):
                                P.op("pe", lambda e, c=c, g=g, cs=cs, w2=w2, bank=bank: e.matmul(psb[bank][:, :], lhsT=w2[:, c, cs], rhs=big[:, c, gs(g)], start=(c == 0), stop=(c == 7)),
                                     reads=[R2, BG[c][g]], writes=[PS[bank]])
                            P.op("dve", lambda e, oc=oc, g=g, bank=bank: e.tensor_tensor(out=xT[:, oc, gs(g)], in0=xT[:, oc, gs(g)], in1=psb[bank][:, :], op=ALU.add),
                                 reads=[PS[bank], X[oc][g]], writes=[X[oc][g]])
            rmsnorm(3 * l + 2, T4, R_T4)
            pTb = big[:, 0:2, :]
            for c in range(2):
                for g in range(NG):
                    wload(pTb[:, c:c + 1, gs(g)], BG[c][g], pT_d[l, c * 128:(c + 1) * 128, gs(g)], 1, 512)
            ev = 0
            for pn in range(4):
                begin_step()
                wpe, Rpe = getw(wd["w_pe"][l, :, pn * 256:(pn + 1) * 256], 2)
                wpg, Rpg = getw(wd["w_pg"][l, :, pn * 256:(pn + 1) * 256], 8)
                for g in range(NG):
                    for ocb in range(2):
                        oc = pn * 2 + ocb
                        cs = slice(ocb * 128, (ocb + 1) * 128)
                        b0 = (ev % 2) * 2
                        ss = ev % 2
                        ev += 1
                        for c in range(2):
                            P.op("pe", lambda e, c=c, g=g, cs=cs, wpe=wpe, b0=b0: e.matmul(psb[b0][:, :], lhsT=wpe[:, c, cs], rhs=pTb[:, c, gs(g)], start=(c == 0), stop=(c == 1)),
                                 reads=[Rpe, BG[c][g]], writes=[PS[b0]])
                        for c in range(8):
                            P.op("pe", lambda e, c=c, g=g, cs=cs, wpg=wpg, b0=b0: e.matmul(psb[b0 + 1][:, :], lhsT=wpg[:, c, cs], rhs=hT[:, c, gs(g)], start=(c == 0), stop=(c == 7)),
                                 reads=[Rpg, H[c][g]], writes=[PS[b0 + 1]])
                        P.op("act", lambda e, ss=ss, b0=b0: e.activation(out=sg[ss], in_=psb[b0 + 1][:, :], func=AF.Sigmoid), reads=[PS[b0 + 1]], writes=[Rsg[ss]])
                        P.op("dve", lambda e, ss=ss, b0=b0: e.tensor_tensor(out=m1[ss], in0=sg[ss], in1=psb[b0][:, :], op=ALU.mult), reads=[Rsg[ss], PS[b0]], writes=[Rm1[ss]])
                        P.op("dve", lambda e, ss=ss, oc=oc, g=g: e.tensor_tensor(out=xT[:, oc, gs(g)], in0=xT[:, oc, gs(g)], in1=m1[ss], op=ALU.add),
                             reads=[Rm1[ss], X[oc][g]], writes=[X[oc][g]])
            ar.reset(m)

        for sidx, stp in enumerate(steps):
            if stp[0] == "A":
                phase_A(stp[1])
            elif stp[0] == "B":
                phase_B(stp[1])
            elif stp[0] == "C":
                phase_C(stp[1])
            elif stp[0] == "F":
                T4f, R_T4f = make_T4()
                rmsnorm(6, T4f, R_T4f, out_final=out_d)
            P.barrier()
        if xout_d is not None:
            xo_v = xout_d.rearrange("(c p) t -> p c t", p=128)
            for c in range(8):
                P.dma("sp", xo_v[:, c, :], xT[:, c, :], reads=[X[c][g] for g in range(NG)], writes=[rd("xout")])
        P.barrier()
        P.emit()
    return nc


def _t5_bucket(dist):
    max_exact = 16
    d = np.maximum(dist, 0)
    df = np.maximum(d, 1).astype(np.float32)
    large = max_exact + (np.log(df / np.float32(max_exact)) / np.float32(np.log(128 / max_exact))
                         * np.float32(32 - max_exact)).astype(np.int32)
    large = np.minimum(large, 31)
    return np.where(d < max_exact, d, large)


def _consts():
    j = np.arange(128)[:, None]
    s = np.arange(128)[None, :]
    ident = (j == s).astype(np.float32)
    negU = np.where(j >= s, -1.0, 0.0).astype(np.float32)
    ones = np.ones((128, 128), np.float32)
    onesM = np.full((128, 128), 1.0 / 1024.0, np.float32)
    return np.ascontiguousarray(np.concatenate([ident, negU, ones, onesM], axis=1))


def _own_tokens(a, hf):
    F = a.shape[-1]
    return a.reshape(16, 2, 128, F)[:, hf].reshape(2048, F)


def _common_maps(inputs):
    g_list = [inputs["g_mix"][0], inputs["g_mlp"][0], inputs["g_pe"][0],
              inputs["g_mix"][1], inputs["g_mlp"][1], inputs["g_pe"][1], inputs["g_final"]]
    gcols = np.zeros((128, 56), np.float32)
    for jx, gv in enumerate(g_list):
        gcols[:, jx * 8:(jx + 1) * 8] = np.asarray(gv, np.float32).reshape(8, 128).T
    sink = np.asarray(inputs["sinks"], np.float32).reshape(1, 16)
    sinkcol = np.ascontiguousarray(np.broadcast_to(sink, (128, 16)))
    i = np.arange(128)[:, None]
    jj = np.arange(256)[None, :]
    dist = 128 + i - jj
    band = (dist >= 0) & (dist < 128)
    bucket = _t5_bucket(dist)
    rb = np.asarray(inputs["rel_bias"], np.float32)
    Bg = rb[bucket]
    Bg = np.where(band[:, :, None], Bg, np.float32(0.0)).transpose(2, 0, 1)
    bandmask = np.where(band, np.float32(0.0), np.float32(NEG))
    per_hf = []
    diag = np.where(np.arange(128)[:, None] >= np.arange(128)[None, :], np.float32(NEG), np.float32(0.0))
    for hf in range(2):
        biasg = np.zeros((8, 128, 384), np.float32)
        maskc = np.full((128, 384), NEG, np.float32)
        o = 0 if hf == 0 else 128
        biasg[:, :, o:o + 256] = Bg
        maskc[:, o:o + 256] = bandmask
        biasg = np.ascontiguousarray(biasg.transpose(1, 0, 2).reshape(128, 8 * 384))
        if hf == 0:
            sbmask = np.concatenate([diag, np.full((128, 128), NEG, np.float32)], axis=1)
        else:
            sbmask = np.concatenate([np.zeros((128, 128), np.float32), diag], axis=1)
        per_hf.append(dict(biasg=biasg, maskc=maskc, sbmask=np.ascontiguousarray(sbmask)))
    return dict(consts=_consts(), gcols=gcols, sinkcol=sinkcol), per_hf


_NC_CACHE = {}


def _get_nc(key, steps, fused=False):
    if key not in _NC_CACHE:
        _NC_CACHE[key] = build(steps, fused=fused)
    return _NC_CACHE[key]


def _f32(a):
    return np.ascontiguousarray(np.asarray(a, np.float32))


FUSED = True


def _assemble(r3):
    out = np.zeros((4, 4096, 1024), np.float32)
    ov = out.reshape(4, 16, 2, 128, 1024)
    for c in range(8):
        b, hf = c // 2, c % 2
        ov[b, :, hf] = np.asarray(r3[c]["outT"], np.float32).T.reshape(16, 128, 1024)
    return out


def kernel(**inputs):
    inputs = {k: np.asarray(v) for k, v in inputs.items()}
    x = _f32(inputs["x"])
    p = _f32(inputs["p"])
    common, per_hf = _common_maps(inputs)
    W = {k: _f32(inputs[k]) for k in W_SHAPES}
    cores = list(range(8))

    def base_map(c):
        b, hf = c // 2, c % 2
        mp = dict(common)
        return mp, b, hf

    xT = []
    pT = []
    for c in cores:
        b, hf = c // 2, c % 2
        xT.append(np.ascontiguousarray(_own_tokens(x[b], hf).T))
        pT.append(np.ascontiguousarray(np.stack([_own_tokens(p[l, b], hf).T for l in range(2)])))

    def gather_pairs(res, name):
        outs = []
        for c in cores:
            b = c // 2
            outs.append(np.ascontiguousarray(np.concatenate([res[2 * b][name], res[2 * b + 1][name]], axis=0)))
        return outs

    def gathered(res, l):
        d = {}
        for pt in range(2):
            d["kxall%d_%d" % (l, pt)] = gather_pairs(res, "kx%d_%d" % (l, pt))
            d["vxall%d_%d" % (l, pt)] = gather_pairs(res, "vx%d_%d" % (l, pt))
        return d

    if FUSED:
        ncf = _get_nc("FUSED", [("A", 0), ("B", 0), ("C", 0), ("A", 1), ("B", 1), ("C", 1), ("F",)], fused=True)
        maps = []
        for c in cores:
            mp, b, hf = base_map(c)
            mp.update(xin=xT[c], pT=pT[c], **per_hf[hf], **W)
            maps.append(mp)
        r3 = run_bass_kernel_spmd(ncf, maps, core_ids=cores).results
        return _assemble(r3)

    nc1 = _get_nc("L1", [("A", 0)])
    maps = []
    for c in cores:
        mp, b, hf = base_map(c)
        mp.update(xin=xT[c], w_in=W["w_in"])
        maps.append(mp)
    r1 = run_bass_kernel_spmd(nc1, maps, core_ids=cores).results
    nc2 = _get_nc("L2", [("B", 0), ("C", 0), ("A", 1)])
    gth = gathered(r1, 0)
    maps = []
    for c in cores:
        mp, b, hf = base_map(c)
        mp.update(xin=xT[c], pT=pT[c], q0=r1[c]["q0"], **{k: v[c] for k, v in gth.items()}, **per_hf[hf], **W)
        maps.append(mp)
    r2 = run_bass_kernel_spmd(nc2, maps, core_ids=cores).results
    nc3 = _get_nc("L3", [("B", 1), ("C", 1), ("F",)])
    gth = gathered(r2, 1)
    maps = []
    for c in cores:
        mp, b, hf = base_map(c)
        mp.update(xin=r2[c]["xout"], pT=pT[c], q1=r2[c]["q1"], **{k: v[c] for k, v in gth.items()}, **per_hf[hf], **W)
        maps.append(mp)
    r3 = run_bass_kernel_spmd(nc3, maps, core_ids=cores).results
    return _assemble(r3)
```

```python
import contextlib
import numpy as np
import ml_dtypes
import concourse.bass as bass
import concourse.mybir as mybir
from concourse.bass_utils import run_bass_kernel_spmd

F32 = mybir.dt.float32
BF16 = mybir.dt.bfloat16
AF = mybir.ActivationFunctionType
ALU = mybir.AluOpType
AX = mybir.AxisListType

ENGINES = ("pe", "act", "dve", "pool", "sp")
NT = 2048
NG = 4
NEG = -30000.0
KEEP_WARM = 1
KEEP_WARM_A = 2
KEEP_WARM_SW = 6
EPS = 1e-6


class Res:
    __slots__ = ("name", "last_write", "reads", "dma_sem", "dma_count")

    def __init__(self, name):
        self.name = name
        self.last_write = None
        self.reads = []
        self.dma_sem = None
        self.dma_count = 0


class Prog:
    def __init__(self, nc):
        self.nc = nc
        self.q = {e: [] for e in ENGINES}
        self.cnt = {e: 0 for e in ENGINES}
        self.seen = {e: {} for e in ENGINES}
        self.dma_res = []

    def _collect(self, eng, reads, writes):
        waits = {}

        def add(tok):
            if tok is None:
                return
            k, v = tok
            if waits.get(k, 0) < v:
                waits[k] = v
        for r in reads:
            add(r.last_write)
        for w in writes:
            add(w.last_write)
            for t in w.reads:
                add(t)
        out = []
        seen = self.seen[eng]
        for k, v in waits.items():
            if k == eng and eng == "pe":
                continue
            if seen.get(k, 0) >= v:
                continue
            seen[k] = v
            out.append((k, v))
        return out

    @staticmethod
    def _commit(tok, reads, writes):
        for r in reads:
            r.reads.append(tok)
            if len(r.reads) > 64:
                best = {}
                for k, v in r.reads:
                    if best.get(k, 0) < v:
                        best[k] = v
                r.reads = list(best.items())
        for w in writes:
            w.last_write = tok
            w.reads = []

    def op(self, eng, fn, reads=(), writes=()):
        waits = self._collect(eng, reads, writes)
        self.cnt[eng] += 1
        tok = (eng, self.cnt[eng])
        self._commit(tok, reads, writes)
        self.q[eng].append((fn, waits, tok))

    def dma(self, eng, out_ap, in_ap, reads=(), writes=(), semres=None):
        sr = semres if semres is not None else writes[0]
        if sr.dma_sem is None:
            sr.dma_sem = "dma%d" % len(self.dma_res)
            self.dma_res.append(sr)
        waits = self._collect(eng, reads, writes)
        sr.dma_count += 16
        tok = (sr.dma_sem, sr.dma_count)
        self._commit(tok, reads, writes)

        def fn(e, out_ap=out_ap, in_ap=in_ap):
            return e.dma_start(out=out_ap, in_=in_ap)
        self.q[eng].append((fn, waits, tok))

    def custom(self, eng, fn, reads, writes, semres):
        if semres.dma_sem is None:
            semres.dma_sem = "cc%d" % len(self.dma_res)
            self.dma_res.append(semres)
        waits = self._collect(eng, reads, writes)
        semres.dma_count += 1
        tok = (semres.dma_sem, semres.dma_count)
        self._commit(tok, reads, writes)
        self.q[eng].append((fn, waits, tok))

    def barrier(self):
        toks = [(e, self.cnt[e]) for e in ENGINES if self.cnt[e] > 0]
        toks += [(r.dma_sem, r.dma_count) for r in self.dma_res if r.dma_count > 0]
        for e in ENGINES:
            waits = []
            for k, v in toks:
                if k == e:
                    if e == "pe":
                        continue
                if self.seen[e].get(k, 0) >= v:
                    continue
                self.seen[e][k] = v
                waits.append((k, v))
            self.q[e].append((None, waits, None))

    def engine_barrier(self, e):
        toks = [(k, self.cnt[k]) for k in ENGINES if self.cnt[k] > 0]
        toks += [(r.dma_sem, r.dma_count) for r in self.dma_res if r.dma_count > 0]
        waits = []
        for k, v in toks:
            if k == e and e == "pe":
                continue
            if self.seen[e].get(k, 0) >= v:
                continue
            self.seen[e][k] = v
            waits.append((k, v))
        self.q[e].append((None, waits, None))

    def emit(self):
        nc = self.nc
        with contextlib.ExitStack() as st:
            sems = {}
            for e in ENGINES:
                sems[e] = st.enter_context(nc.semaphore("s_" + e))
            for r in self.dma_res:
                sems[r.dma_sem] = st.enter_context(nc.semaphore("s_" + r.dma_sem))
            block = st.enter_context(nc.Block())

            def run(eng_name, eng):
                for fn, waits, tok in self.q[eng_name]:
                    for k, v in waits:
                        eng.wait_ge(sems[k], v)
                    if fn is None:
                        continue
                    ins = fn(eng)
                    if tok is not None:
                        k, v = tok
                        ins.then_inc(sems[k], 16 if k.startswith("dma") else 1)

            @block.tensor
            def _(e):
                run("pe", e)

            @block.scalar
            def _(e):
                run("act", e)

            @block.vector
            def _(e):
                run("dve", e)

            @block.gpsimd
            def _(e):
                run("pool", e)

            @block.sync
            def _(e):
                run("sp", e)


class Arena:
    def __init__(self, tensor, nbytes):
        self.t = tensor
        self.nbytes = nbytes
        self.off = 0

    def alloc(self, cols, dtype):
        esz = 4 if dtype == F32 else 2
        nb = cols * esz
        nb4 = (nb + 63) // 64 * 64
        assert self.off + nb4 <= self.nbytes, ("arena overflow", self.off, nb4, self.nbytes)
        a = self.t[:, self.off // 4:(self.off + nb4) // 4]
        self.off += nb4
        if dtype != F32:
            a = a.bitcast(dtype)
        return a[:, 0:cols]

    def mark(self):
        return self.off

    def reset(self, m):
        self.off = m


W_SHAPES = {
    "w_in": [2, 1024, 4352], "w_up_a": [2, 512, 1024], "w_up_b": [2, 512, 1024],
    "w_o": [2, 1024, 1024], "w_ff1": [2, 1024, 4096], "w_ff2": [2, 4096, 1024],
    "w_pe": [2, 256, 1024], "w_pg": [2, 1024, 1024],
}


def build(steps, fused=False, n_cores=8, debug=False):
    nc = bass.Bass("TRN2", target_bir_lowering=False)
    P = Prog(nc)
    kinds = [s[0] for s in steps]
    has = lambda k, l: (k, l) in steps
    layers = sorted({s[1] for s in steps if len(s) > 1})

    def din(name, shape, dt=F32):
        return nc.dram_tensor(name, shape, dt, kind="ExternalInput").ap()

    def dout(name, shape, dt=F32):
        return nc.dram_tensor(name, shape, dt, kind="ExternalOutput").ap()

    def dint(name, shape, dt=F32):
        return nc.dram_tensor(name, shape, dt, kind="Internal").ap()

    consts_d = din("consts", [128, 4 * 128])
    gcols_d = din("gcols", [128, 56])
    sink_d = din("sinkcol", [128, 16])
    need_x_in = True
    xin_d = din("xin", [1024, NT])
    final = ("F",) in steps
    out_d = dout("outT", [1024, NT]) if final else None
    xout_d = None
    if not final and any(k == "C" for k in kinds):
        xout_d = dout("xout", [1024, NT])
    wd = {}
    needA = any(k == "A" for k in kinds)
    needC = any(k == "C" for k in kinds)
    needB = any(k == "B" for k in kinds)
    if needA:
        wd["w_in"] = din("w_in", W_SHAPES["w_in"])
    if needC:
        for k in W_SHAPES:
            if k not in wd:
                wd[k] = din(k, W_SHAPES[k])
        pT_d = din("pT", [2, 256, NT])
    if needB:
        biasg_d = din("biasg", [128, 8 * 384])
        maskc_d = din("maskc", [128, 384])
        sbmask_d = din("sbmask", [128, 256])
    q_d, kx_d, vx_d, kxall_d, vxall_d = {}, {}, {}, {}, {}
    for l in layers:
        a_here, b_here = has("A", l), has("B", l)
        if fused:
            q_d[l] = dint("q%d" % l, [1024, NT], BF16)
            kx_d[l] = [dint("kx%d_%d" % (l, pt), [384, NT], BF16) for pt in range(2)]
            vx_d[l] = [dint("vx%d_%d" % (l, pt), [NT // 2, 640], BF16) for pt in range(2)]
            kxall_d[l] = [dint("kxall%d_%d" % (l, pt), [2 * 384, NT], BF16) for pt in range(2)]
            vxall_d[l] = [dint("vxall%d_%d" % (l, pt), [NT, 640], BF16) for pt in range(2)]
        else:
            if a_here:
                q_d[l] = (dint if b_here else dout)("q%d" % l, [1024, NT], BF16)
                kx_d[l] = [dout("kx%d_%d" % (l, pt), [384, NT], BF16) for pt in range(2)]
                vx_d[l] = [dout("vx%d_%d" % (l, pt), [NT // 2, 640], BF16) for pt in range(2)]
            if b_here:
                if not a_here:
                    q_d[l] = din("q%d" % l, [1024, NT], BF16)
                kxall_d[l] = [din("kxall%d_%d" % (l, pt), [2 * 384, NT], BF16) for pt in range(2)]
                vxall_d[l] = [din("vxall%d_%d" % (l, pt), [NT, 640], BF16) for pt in range(2)]
    ot_d = (dout if debug else dint)("ot", [1024, NT], BF16) if needB or needC else None

    st = contextlib.ExitStack()
    with st:
        xT = st.enter_context(nc.sbuf_tensor("xT_sb", [128, 8, NT], F32))
        hT = st.enter_context(nc.sbuf_tensor("hT", [128, 8, NT], BF16))
        cst = st.enter_context(nc.sbuf_tensor("cst", [128, 4 * 128], BF16))
        cstf = st.enter_context(nc.sbuf_tensor("cstf", [128, 4 * 128], F32))
        gcols = st.enter_context(nc.sbuf_tensor("gcols_sb", [128, 56], F32))
        sinkc = st.enter_context(nc.sbuf_tensor("sinkc_sb", [128, 16], F32))
        onec = st.enter_context(nc.sbuf_tensor("onec", [128, 2], F32))
        ARENA_BYTES = 90 * 1024
        scr = st.enter_context(nc.sbuf_tensor("scr", [128, ARENA_BYTES // 4], F32))
        ar = Arena(scr, ARENA_BYTES)
        psall_t = st.enter_context(nc.psum_tensor("psall", [128, 7 * 512], F32))
        psall = psall_t[:, :]
        psb = [psall[:, i * 512:(i + 1) * 512] for i in range(7)]
        pst_t = st.enter_context(nc.psum_tensor("pst", [128, 1024], BF16))
        pst = pst_t[:, :]
        PS = [Res("ps%d" % i) for i in range(7)]
        PST = Res("pst")

        ident = cst[:, 0:128]
        negU = cst[:, 128:256]
        onesB = cst[:, 256:384]
        onesM = cst[:, 384:512]

        X = [[Res("x%d_%d" % (c, g)) for g in range(NG)] for c in range(8)]
        H = [[Res("h%d_%d" % (c, g)) for g in range(NG)] for c in range(8)]
        R_cst, R_cstf, R_g, R_sink, R_one = Res("cst"), Res("cstf"), Res("gcols"), Res("sink"), Res("one")
        R_dram = {}

        def rd(name):
            if name not in R_dram:
                R_dram[name] = Res("d_" + name)
            return R_dram[name]

        def gs(g):
            return slice(g * 512, (g + 1) * 512)

        P.dma("sp", cstf[:], consts_d, writes=[R_cstf])
        P.dma("sp", gcols[:], gcols_d, writes=[R_g])
        P.dma("sp", sinkc[:], sink_d, writes=[R_sink])
        P.op("pool", lambda e: e.tensor_copy(out=cst[:], in_=cstf[:]), reads=[R_cstf], writes=[R_cst])
        P.op("pool", lambda e: e.memset(onec[:, 0:1], 1.0), writes=[R_one])
        P.op("pool", lambda e: e.memset(onec[:, 1:2], EPS), writes=[R_one])
        xin_v = xin_d.rearrange("(c p) t -> p c t", p=128)
        for c in range(8):
            P.dma("sp", xT[:, c, :], xin_v[:, c, :], writes=[X[c][g] for g in range(NG)])

        stg_state = {"i": 0}

        def make_wloader(nslots_stg=2, cast_eng="pool"):
            stgs = [ar.alloc(1024, F32) for _ in range(nslots_stg)]
            rs = [Res("stg%d" % i) for i in range(nslots_stg)]

            def wload1(dst3, dst_res, src2, kc, ncols):
                i = stg_state["i"] % nslots_stg
                stg_state["i"] += 1
                sv = stgs[i][:, 0:kc * ncols].rearrange("p (c n) -> p c n", c=kc)
                P.dma("sp", sv, src2.rearrange("(c p) n -> p c n", p=128), writes=[rs[i]])
                P.op(cast_eng, lambda e: e.tensor_copy(out=dst3, in_=sv), reads=[rs[i]], writes=[dst_res])

            def wload(dst3, dst_res, src2, kc, ncols):
                kcc = max(1, 1024 // ncols)
                for c0 in range(0, kc, kcc):
                    c1 = min(kc, c0 + kcc)
                    wload1(dst3[:, c0:c1, :], dst_res, src2[c0 * 128:c1 * 128, :], c1 - c0, ncols)
            return wload

        def make_T4():
            return [ar.alloc(512, F32) for _ in range(4)], [Res("T4_%d" % i) for i in range(4)]

        def rmsnorm(gidx, T4, R_T4, out_final=None):
            m = ar.mark()
            sq = [T4[2].bitcast(BF16)[:, 0:512], T4[3].bitcast(BF16)[:, 0:512]]
            Rsq = [R_T4[2], R_T4[3]]
            rstd = [T4[0], T4[1]]
            Rr = [R_T4[0], R_T4[1]]
            if out_final is not None:
                ofl = [ar.alloc(512, F32) for _ in range(2)]
                Ro = [Res("of0"), Res("of1")]
            k = 0
            for g in range(NG):
                bank = g % 2
                for c in range(8):
                    s = k % 2
                    k += 1
                    P.op("act", lambda e, s=s, c=c, g=g: e.activation(out=sq[s], in_=xT[:, c, gs(g)], func=AF.Square),
                         reads=[X[c][g]], writes=[Rsq[s]])
                    P.op("pe", lambda e, s=s, c=c, bank=bank: e.matmul(psb[bank][:, :], lhsT=onesM, rhs=sq[s], start=(c == 0), stop=(c == 7)),
                         reads=[Rsq[s], R_cst], writes=[PS[bank]])
                P.op("act", lambda e, bank=bank: e.activation(out=rstd[bank], in_=psb[bank][:, :], func=AF.Ln, bias=onec[:, 1:2], scale=1.0),
                     reads=[PS[bank], R_one], writes=[Rr[bank]])
                P.op("act", lambda e, bank=bank: e.activation(out=rstd[bank], in_=rstd[bank], func=AF.Exp, scale=-0.5),
                     reads=[Rr[bank]], writes=[Rr[bank]])
                for c in range(8):
                    gcol = gcols[:, gidx * 8 + c: gidx * 8 + c + 1]
                    if out_final is None:
                        P.op("dve", lambda e, c=c, g=g, bank=bank, gcol=gcol: e.scalar_tensor_tensor(
                            out=hT[:, c, gs(g)], in0=xT[:, c, gs(g)], scalar=gcol, in1=rstd[bank], op0=ALU.mult, op1=ALU.mult),
                            reads=[X[c][g], Rr[bank], R_g], writes=[H[c][g]])
                    else:
                        s = (g * 8 + c) % 2
                        P.op("dve", lambda e, c=c, g=g, bank=bank, gcol=gcol, s=s: e.scalar_tensor_tensor(
                            out=ofl[s], in0=xT[:, c, gs(g)], scalar=gcol, in1=rstd[bank], op0=ALU.mult, op1=ALU.mult),
                            reads=[X[c][g], Rr[bank], R_g], writes=[Ro[s]])
                        P.dma("act", out_final[c * 128:(c + 1) * 128, gs(g)], ofl[s], reads=[Ro[s]], writes=[rd("out")], semres=Ro[s])
            ar.reset(m)

        def phase_A(l):
            m = ar.mark()
            R_dummyA = Res("dummyA")
            T4, R_T4 = make_T4()
            rmsnorm(3 * l + 0, T4, R_T4)
            wload = make_wloader(4, cast_eng="dve")
            w_in = wd["w_in"]
            NW = 5
            wp = [ar.alloc(8 * 256, BF16).rearrange("p (c n) -> p c n", c=8) for _ in range(NW)]
            Rw = [Res("wpA%d" % i) for i in range(NW)]
            ot = [ar.alloc(512, BF16) for _ in range(4)]
            Rot = [Res("otA%d" % i) for i in range(4)]
            qd, kd, vd = q_d[l], kx_d[l], vx_d[l]
            wv = ar.alloc(8 * 640, BF16).rearrange("p (c n) -> p c n", c=8)
            Rwv = Res("wv")
            wload(wv[:, :, 0:256], Rwv, w_in[l, :, 1024:1280], 8, 256)
            wload(wv[:, :, 256:512], Rwv, w_in[l, :, 1280:1536], 8, 256)
            wload(wv[:, :, 512:640], Rwv, w_in[l, :, 2176:2304], 8, 128)
            vt = [ar.alloc(640, BF16) for _ in range(2)]
            Rvt = [Res("vt0"), Res("vt1")]
            for tb in range(16):
                g = tb // 4
                b0, b1 = (0, 1) if tb % 2 == 0 else (2, 3)
                for c in range(8):
                    lhs = hT[:, c, tb * 128:(tb + 1) * 128]
                    P.op("pe", lambda e, c=c, lhs=lhs, b0=b0: e.matmul(psb[b0][:, :], lhsT=lhs, rhs=wv[:, c, 0:512], start=(c == 0), stop=(c == 7)),
                         reads=[Rwv, H[c][g]], writes=[PS[b0]])
                    P.op("pe", lambda e, c=c, lhs=lhs, b1=b1: e.matmul(psb[b1][:, 0:128], lhsT=lhs, rhs=wv[:, c, 512:640], start=(c == 0), stop=(c == 7)),
                         reads=[Rwv, H[c][g]], writes=[PS[b1]])
                for _ in range(KEEP_WARM_A):
                    P.op("pe", lambda e: e.matmul(psb[6][:, 0:512], lhsT=onesB, rhs=cst[:, 0:512], start=True, stop=True),
                         reads=[R_cst], writes=[R_dummyA])
                s = tb % 2
                P.op("act", lambda e, s=s, b0=b0: e.activation(out=vt[s][:, 0:512], in_=psb[b0][:, :], func=AF.Copy),
                     reads=[PS[b0]], writes=[Rvt[s]])
                P.op("act", lambda e, s=s, b1=b1: e.activation(out=vt[s][:, 512:640], in_=psb[b1][:, 0:128], func=AF.Copy),
                     reads=[PS[b1]], writes=[Rvt[s]])
                vdd = vd[tb // 8]
                P.dma("act", vdd[(tb % 8) * 128:(tb % 8 + 1) * 128, :], vt[s], reads=[Rvt[s]], writes=[rd(vdd.tensor.name)], semres=Rvt[s])
            exchange_parts("v", l)
            panels = []
            for pn in range(2):
                panels.append((512 + pn * 256, kd, pn * 2, 1.0, False))
            panels.append((2048, kd, 4, 1.0, True))
            for pn in range(2):
                panels.append((pn * 256, qd, pn * 2, 0.125, False))
            for pn in range(2):
                panels.append((1536 + pn * 256, qd, 4 + pn * 2, 0.125, False))
            k = 0
            ev = 0
            def load_panel(q):
                (col0, dst, rb0, scale, dup) = panels[q]
                s = q % NW
                if not dup:
                    wload(wp[s], Rw[s], w_in[l, :, col0:col0 + 256], 8, 256)
                else:
                    for kv in range(2):
                        for dd in range(2):
                            wload(wp[s][:, :, (2 * kv + dd) * 64:(2 * kv + dd + 1) * 64], Rw[s],
                                  w_in[l, :, col0 + kv * 64: col0 + (kv + 1) * 64], 8, 64)
            pl = {"n": 0}
            for pidx, (col0, dst, rb0, scale, dup) in enumerate(panels):
                if pidx == 3:
                    exchange_parts("k", l)
                s = k % NW
                k += 1
                while pl["n"] < min(len(panels), pidx + 4):
                    load_panel(pl["n"])
                    pl["n"] += 1
                for ocb in range(2):
                    for g in range(NG):
                        bank = 2 + (ev % 4)
                        for c in range(8):
                            P.op("pe", lambda e, s=s, c=c, g=g, ocb=ocb, bank=bank: e.matmul(
                                psb[bank][:, :], lhsT=wp[s][:, c, ocb * 128:(ocb + 1) * 128], rhs=hT[:, c, gs(g)],
                                start=(c == 0), stop=(c == 7)),
                                reads=[Rw[s], H[c][g]], writes=[PS[bank]])
                        for _ in range(KEEP_WARM_A):
                            P.op("pe", lambda e: e.matmul(psb[6][:, 0:512], lhsT=onesB, rhs=cst[:, 0:512], start=True, stop=True),
                                 reads=[R_cst], writes=[R_dummyA])
                        o = ev % 4
                        if ev % 2 == 0:
                            P.op("act", lambda e, o=o, bank=bank, scale=scale: e.activation(out=ot[o], in_=psb[bank][:, :], func=AF.Copy, scale=scale),
                                 reads=[PS[bank]], writes=[Rot[o]])
                        else:
                            P.op("dve", lambda e, o=o, bank=bank, scale=scale: e.tensor_scalar(out=ot[o], in0=psb[bank][:, :], scalar1=scale, scalar2=None, op0=ALU.mult),
                                 reads=[PS[bank]], writes=[Rot[o]])
                        rb = rb0 + ocb
                        if dst is kd:
                            dd = kd[rb // 3]
                            rb = rb % 3
                        else:
                            dd = dst
                        P.dma("act", dd[rb * 128:(rb + 1) * 128, gs(g)], ot[o], reads=[Rot[o]], writes=[rd(dd.tensor.name)], semres=Rot[o])
                        ev += 1
            ar.reset(m)

        def exchange_parts(kind, l):
            if not fused:
                return
            pairs = list(zip(kx_d[l], kxall_d[l])) if kind == "k" else list(zip(vx_d[l], vxall_d[l]))
            groups = [[2 * i, 2 * i + 1] for i in range(n_cores // 2)]
            P.engine_barrier("pool")
            for src, dst in pairs:
                P.custom("pool", lambda e, src=src, dst=dst: e.collective_compute(
                    "AllGather", ALU.bypass, replica_groups=groups, ins=[src.opt()], outs=[dst.opt()]),
                    reads=[rd(src.tensor.name)], writes=[rd(dst.tensor.name)], semres=rd(dst.tensor.name))

        def phase_B(l):
            m = ar.mark()
            qd, kall, vall = q_d[l], kxall_d[l], vxall_d[l]
            Rq = rd(qd.tensor.name)
            Rk = [rd(t.tensor.name) for t in kall]
            Rv = [rd(t.tensor.name) for t in vall]

            def krows(r, blk):
                t = kall[blk // 3]
                o = r * 384 + (blk % 3) * 128
                return t[o:o + 128, :]
            Rot_d = rd("ot")
            sbm_f = ar.alloc(256, F32)
            sbm = ar.alloc(256, BF16)
            R_sbmf, R_sbm = Res("sbmf"), Res("sbm")
            P.dma("sp", sbm_f, sbmask_d, writes=[R_sbmf])
            P.op("pool", lambda e: e.tensor_copy(out=sbm, in_=sbm_f), reads=[R_sbmf], writes=[R_sbm])
            maskLo, maskHi = sbm[:, 0:128], sbm[:, 128:256]
            bm = ar.alloc(8 * 384, F32)
            mk = ar.alloc(384, F32)
            R_bm, R_mk = Res("bm"), Res("mk")
            P.dma("sp", bm, biasg_d, writes=[R_bm])
            P.dma("sp", mk, maskc_d, writes=[R_mk])
            bm3 = bm.rearrange("p (h n) -> p h n", h=8)
            for h in range(8):
                P.op("pool", lambda e, h=h: e.tensor_tensor(out=bm3[:, h, :], in0=bm3[:, h, :], in1=mk, op=ALU.add),
                     reads=[R_mk, R_bm], writes=[R_bm])
            KT = [ar.alloc(4096, BF16) for _ in range(2)]
            VV = [ar.alloc(32 * 128, BF16).rearrange("p (s n) -> p s n", s=32) for _ in range(2)]
            QQ = [ar.alloc(NT, BF16) for _ in range(2)]
            RK, RV, RQ = [Res("KT0"), Res("KT1")], [Res("VV0"), Res("VV1")], [Res("QQ0"), Res("QQ1")]
            osb = [ar.alloc(512, BF16) for _ in range(2)]
            Rosb = [Res("osb0"), Res("osb1")]
            vall_v = [t.rearrange("(s p) n -> p s n", p=128) for t in vall]

            m2 = ar.mark()
            E_t = [ar.alloc(512, F32) for _ in range(2)]
            SP_t = [ar.alloc(512, BF16) for _ in range(3)]
            ARG_t = [ar.alloc(512, F32) for _ in range(2)]
            AT_t = [ar.alloc(512, BF16) for _ in range(3)]
            RB_t = [[ar.alloc(512, F32) for _ in range(2)] for _ in range(2)]
            R_E = [Res("E0"), Res("E1")]
            R_SP = [Res("SP%d" % i) for i in range(3)]
            R_ARG = [Res("ARG0"), Res("ARG1")]
            R_AT = [Res("AT%d" % i) for i in range(3)]
            R_RB = [[Res("RB%d%d" % (h, q)) for q in range(2)] for h in range(2)]
            ZB = [0, 1, 2]
            RPB = [3, 4]
            OB = [5, 6]

            units = []
            ocount = 0
            def sb_loads(pr):
                s = pr % 2
                P.dma("sp", QQ[s], qd[pr * 128:(pr + 1) * 128, :], reads=[Rq], writes=[RQ[s]])
                for r in range(2):
                    P.dma("sp", KT[s][:, r * 2048:(r + 1) * 2048], krows(r, pr), reads=Rk, writes=[RK[s]])
                for pt in range(2):
                    for r in range(2):
                        P.dma("sp", VV[s][:, r * 16 + pt * 8: r * 16 + pt * 8 + 8, :], vall_v[pt][:, r * 8:(r + 1) * 8, pr * 128:(pr + 1) * 128],
                              reads=Rv, writes=[RV[s]])

            first_idx = {}
            for pr in range(4):
                s = pr % 2
                first_idx[pr] = len(units)
                for mg in range(4):
                    ob = OB[ocount % 2]
                    osel = ocount % 2
                    ocount += 1
                    first = True
                    nk = 8 * mg + 8
                    for kb in range(nk - 1, -1, -1):
                        c0 = max(0, (kb - 8 * mg) // 2) if kb >= 8 * mg else 0
                        c0 = 0
                        while 2 * (4 * mg + c0) + 1 < kb:
                            c0 += 1
                        mask = None
                        for c in range(c0, 4):
                            lo = 2 * (4 * mg + c)
                            if kb == lo + 1:
                                mask = (c, maskHi)
                            elif kb == lo:
                                mask = (c, maskLo)
                        for h in range(2):
                            units.append(dict(pr=pr, s=s, mg=mg, kb=kb, c0=c0, mask=mask, h=h, ob=ob, osel=osel,
                                              first=(kb == nk - 1), last=(kb == 0)))

            def kcols(kb):
                r, j = kb % 2, kb // 2
                return slice(r * 2048 + j * 128, r * 2048 + (j + 1) * 128)

            def vslot(kb):
                r, j = kb % 2, kb // 2
                return r * 16 + j

            def s1(i, u):
                h, s, c0 = u["h"], u["s"], u["c0"]
                hp = slice(64 * h, 64 * h + 64)
                zb = ZB[i % 3]
                a0 = c0 * 128
                qcols = slice(u["mg"] * 512 + a0, (u["mg"] + 1) * 512)
                if u["first"]:
                    rb = RB_t[h][u["mg"] % 2]
                    P.op("pool", lambda e, rb=rb: e.memset(rb, 0.0), writes=[R_RB[h][u["mg"] % 2]])
                    if h == 0:
                        P.op("dve", lambda e, ob=u["ob"]: e.memset(psb[ob][:, :], 0.0), writes=[PS[u["ob"]]])
                has_mask = u["mask"] is not None
                P.op("pe", lambda e: e.matmul(psb[zb][:, a0:512], lhsT=KT[s][hp, kcols(u["kb"])], rhs=QQ[s][hp, qcols],
                                              start=True, stop=(not has_mask)),
                     reads=[RK[s], RQ[s]], writes=[PS[zb]])
                if has_mask:
                    c, mk_ap = u["mask"]
                    P.op("pe", lambda e: e.matmul(psb[zb][:, c * 128:(c + 1) * 128], lhsT=ident, rhs=mk_ap, start=False, stop=True),
                         reads=[R_cst, R_sbm], writes=[PS[zb]])

            def s1_act(i, u):
                c0 = u["c0"]
                zb = ZB[i % 3]
                a0 = c0 * 128
                et, spt = E_t[i % 2], SP_t[i % 3]
                P.op("act", lambda e: e.activation(out=et[:, a0:512], in_=psb[zb][:, a0:512], func=AF.Exp),
                     reads=[PS[zb]], writes=[R_E[i % 2]])

            def s1_act2(i, u):
                c0 = u["c0"]
                a0 = c0 * 128
                et, spt = E_t[i % 2], SP_t[i % 3]
                P.op("act", lambda e: e.activation(out=spt[:, a0:512], in_=et[:, a0:512], func=AF.Ln, bias=1.0, scale=1.0),
                     reads=[R_E[i % 2], R_one], writes=[R_SP[i % 3]])

            def s2(i, u):
                h, c0 = u["h"], u["c0"]
                zb, rpb = ZB[i % 3], RPB[i % 2]
                a0 = c0 * 128
                spt = SP_t[i % 3]
                rb, Rrb = RB_t[h][u["mg"] % 2], R_RB[h][u["mg"] % 2]
                P.op("pe", lambda e: e.matmul(psb[zb][:, a0:512], lhsT=negU, rhs=spt[:, a0:512], start=False, stop=True, skip_group_check=True),
                     reads=[R_SP[i % 3], R_cst], writes=[PS[zb]])
                P.op("pe", lambda e: e.matmul(psb[rpb][:, a0:512], lhsT=onesB, rhs=spt[:, a0:512], start=True, stop=True),
                     reads=[R_SP[i % 3], R_cst], writes=[PS[rpb]])

            def s2_dve(i, u):
                h, c0 = u["h"], u["c0"]
                zb, rpb = ZB[i % 3], RPB[i % 2]
                a0 = c0 * 128
                rb, Rrb = RB_t[h][u["mg"] % 2], R_RB[h][u["mg"] % 2]
                argt = ARG_t[i % 2]
                P.op("dve", lambda e: e.tensor_tensor(out=argt[:, a0:512], in0=psb[zb][:, a0:512], in1=rb[:, a0:512], op=ALU.subtract),
                     reads=[PS[zb], Rrb], writes=[R_ARG[i % 2]])
                if not u["last"]:
                    P.op("dve", lambda e: e.tensor_tensor(out=rb[:, a0:512], in0=rb[:, a0:512], in1=psb[rpb][:, a0:512], op=ALU.add),
                         reads=[PS[rpb], Rrb], writes=[Rrb])

            def s3(i, u):
                h, s, c0 = u["h"], u["s"], u["c0"]
                a0 = c0 * 128
                argt, att = ARG_t[i % 2], AT_t[i % 3]
                ob = u["ob"]
                P.op("act", lambda e: e.activation(out=att[:, a0:512], in_=argt[:, a0:512], func=AF.Exp),
                     reads=[R_ARG[i % 2]], writes=[R_AT[i % 3]])

            def s3_pe(i, u):
                h, s, c0 = u["h"], u["s"], u["c0"]
                a0 = c0 * 128
                att = AT_t[i % 3]
                ob = u["ob"]
                vs = vslot(u["kb"])
                kw = dict(tile_position=(0, 64)) if h == 1 else {}
                P.op("pe", lambda e: e.matmul(psb[ob][64 * h:64 * h + 64, a0:512], lhsT=VV[s][:, vs, 64 * h:64 * h + 64],
                                              rhs=att[:, a0:512], start=False, stop=u["last"], skip_group_check=True, **kw),
                     reads=[R_AT[i % 3], RV[s]], writes=[PS[ob]])
                if u["last"] and h == 1:
                    osel = u["osel"]
                    P.op("act", lambda e: e.activation(out=osb[osel], in_=psb[ob][:, :], func=AF.Copy),
                         reads=[PS[ob]], writes=[Rosb[osel]])
                    P.dma("act", ot_d[u["pr"] * 128:(u["pr"] + 1) * 128, gs(u["mg"])], osb[osel], reads=[Rosb[osel]], writes=[Rot_d], semres=Rosb[osel])

            n = len(units)
            sb_loads(0)
            sb_loads(1)
            load_at = {first_idx[pr] + 6: pr + 1 for pr in range(1, 3)}
            pstF = pst.bitcast(F32)
            R_dummy = Res("dummy")

            def keep_warm():
                P.op("pe", lambda e: e.matmul(pstF[:, 0:512], lhsT=onesB, rhs=cst[:, 0:512], start=True, stop=True),
                     reads=[R_cst], writes=[R_dummy])

            s1(0, units[0])
            for i in range(n + 3):
                if i in load_at:
                    sb_loads(load_at[i])
                if 0 <= i - 1 < n:
                    s2(i - 1, units[i - 1])
                if i + 1 < n:
                    s1(i + 1, units[i + 1])
                if i < n:
                    for _ in range(KEEP_WARM + (1 if units[i]["c0"] >= 1 else 0)):
                        keep_warm()
                if 0 <= i - 3 < n:
                    s3_pe(i - 3, units[i - 3])
                if i < n:
                    s1_act(i, units[i])
                if 0 <= i - 1 < n:
                    s2_dve(i - 1, units[i - 1])
                if 0 <= i - 2 < n:
                    s3(i - 2, units[i - 2])
                if i < n:
                    s1_act2(i, units[i])

            P.barrier()
            ar.reset(m2)
            bm4 = bm.rearrange("p (q t n) -> p q t n", q=4, t=2)[:, :, 0, :]
            bsp = [ar.alloc(4 * 384, BF16).rearrange("p (q n) -> p q n", q=4) for _ in range(3)]
            bres = KT[1].bitcast(F32)[:, 0:4 * 384].rearrange("p (q n) -> p q n", q=4)
            R_bhl = Res("bhl")
            P.op("dve", lambda e: e.tensor_copy(out=bsp[0], in_=bm4), reads=[R_bm], writes=[R_bhl])
            P.op("dve", lambda e: e.tensor_tensor(out=bres, in0=bm4, in1=bsp[0], op=ALU.subtract), reads=[R_bm, R_bhl], writes=[R_bhl, RK[1]])
            P.op("dve", lambda e: e.tensor_copy(out=bsp[1], in_=bres), reads=[R_bhl, RK[1]], writes=[R_bhl])
            P.op("dve", lambda e: e.tensor_tensor(out=bres, in0=bres, in1=bsp[1], op=ALU.subtract), reads=[R_bhl, RK[1]], writes=[R_bhl, RK[1]])
            P.op("dve", lambda e: e.tensor_copy(out=bsp[2], in_=bres), reads=[R_bhl, RK[1]], writes=[R_bhl])
            negsink = ar.alloc(8, F32)
            R_ns = Res("negsink")
            P.op("dve", lambda e: e.tensor_scalar(out=negsink, in0=sinkc[:, l * 8:(l + 1) * 8], scalar1=-1.0, scalar2=None, op0=ALU.mult),
                 reads=[R_sink], writes=[R_ns])
            NB = 3
            Pf = [ar.alloc(768, F32).rearrange("p (h n) -> p h n", h=2) for _ in range(NB)]
            Pn = [ar.alloc(768, BF16).rearrange("p (h n) -> p h n", h=2) for _ in range(NB)]
            PTs = [ar.alloc(768, BF16) for _ in range(NB)]
            sm = [ar.alloc(16, F32) for _ in range(NB)]
            R_P = [Res("P%d" % i) for i in range(NB)]
            R_Pn = [Res("Pn%d" % i) for i in range(NB)]
            R_PT = [Res("PT%d" % i) for i in range(NB)]
            R_smA = [Res("smA%d" % i) for i in range(NB)]
            R_smB = [Res("smB%d" % i) for i in range(NB)]
            R_smC = [Res("smC%d" % i) for i in range(NB)]
            R_smD = [Res("smD%d" % i) for i in range(NB)]
            SBK = [(0, 1), (2, 3)]
            PTP = [pst[:, 0:768], psb[4].bitcast(BF16)[:, 0:768]]
            R_PTP = [PST, PS[4]]
            OBW = [5, 6]

            def sw_kv_loads(kv):
                s = kv % 2
                for r in range(2):
                    P.dma("sp", KT[s][:, r * 2048:(r + 1) * 2048], krows(r, 4 + kv), reads=Rk, writes=[RK[s]])
                for pt in range(2):
                    for r in range(2):
                        P.dma("sp", VV[s][:, r * 16 + pt * 8: r * 16 + pt * 8 + 8, 0:64],
                              vall_v[pt][:, r * 8:(r + 1) * 8, 512 + kv * 64: 512 + (kv + 1) * 64], reads=Rv, writes=[RV[s]])

            def sw_q_load(qi):
                P.dma("sp", QQ[qi % 2], qd[(4 + qi) * 128:(5 + qi) * 128, :], reads=[Rq], writes=[RQ[qi % 2]])

            swu = []
            ocw = 0
            for kv in range(2):
                for pp in range(2):
                    qi = 2 * kv + pp
                    for jg in range(4):
                        ob = OBW[ocw % 2]
                        osel = ocw % 2
                        ocw += 1
                        for jj in range(4):
                            swu.append(dict(kv=kv, pp=pp, qi=qi, jg=jg, jj=jj, j=4 * jg + jj, ob=ob, osel=osel,
                                            firstq=(jg == 0 and jj == 0)))

            def wins_of(j):
                w = [(1, 0, j), (2, 1, j)]
                if j > 0:
                    w = [(0, 1, j - 1)] + w
                return w

            def w1(i, u):
                kv, qi, j = u["kv"], u["qi"], u["j"]
                s, qs, b = kv % 2, qi % 2, i % NB
                if u["firstq"] and qi + 1 < 4 and qi + 1 >= 2:
                    sw_q_load(qi + 1)
                wins = wins_of(j)
                w0 = wins[0][0] * 128
                head0 = 4 * kv + 2 * u["pp"]
                if u["jj"] == 0:
                    P.op("dve", lambda e: e.memset(psb[u["ob"]][:, :], 0.0), writes=[PS[u["ob"]]])
                smt = sm[b]
                for half in range(2):
                    hp = slice(64 * half, 64 * half + 64)
                    bank = SBK[i % 2][half]
                    head = head0 + half
                    for wi, (w, r, sl) in enumerate(wins):
                        P.op("pe", lambda e, w=w, r=r, sl=sl, hp=hp, bank=bank, wi=wi: e.matmul(
                            psb[bank][:, w * 128:(w + 1) * 128], lhsT=QQ[qs][hp, j * 128:(j + 1) * 128],
                            rhs=KT[s][hp, r * 2048 + sl * 128: r * 2048 + (sl + 1) * 128], start=(wi == 0),
                            stop=True, skip_group_check=(wi > 0)),
                            reads=[RQ[qs], RK[s]], writes=[PS[bank]])
                    if half == 0:
                        for tsp in range(3):
                            P.op("pe", lambda e, bank=bank, head=head, tsp=tsp: e.matmul(
                                psb[bank][:, w0:384], lhsT=ident, rhs=bsp[tsp][:, head // 2, w0:384], start=False, stop=True, skip_group_check=True),
                                reads=[R_cst, R_bhl], writes=[PS[bank]])

            def w1_rest(i, u):
                kv, qi, j = u["kv"], u["qi"], u["j"]
                b = i % NB
                wins = wins_of(j)
                w0 = wins[0][0] * 128
                head0 = 4 * kv + 2 * u["pp"]
                smt = sm[b]
                bk0, bk1 = SBK[i % 2]
                P.op("dve", lambda e: e.tensor_tensor(out=Pf[b][:, 1, w0:384], in0=psb[bk1][:, w0:384], in1=bm3[:, head0 + 1, w0:384], op=ALU.add),
                     reads=[PS[bk1], R_bm], writes=[R_P[b]])
                P.op("dve", lambda e: e.reduce_max(out=smt[:, 0:1], in_=psb[bk0][:, w0:384], axis=AX.X), reads=[PS[bk0]], writes=[R_smA[b]])
                P.op("dve", lambda e: e.reduce_max(out=smt[:, 1:2], in_=Pf[b][:, 1, w0:384], axis=AX.X), reads=[R_P[b]], writes=[R_smA[b]])
                P.op("dve", lambda e: e.scalar_tensor_tensor(out=smt[:, 2:4], in0=smt[:, 0:2], scalar=-1.0, in1=negsink[:, head0:head0 + 2],
                                                             op0=ALU.mult, op1=ALU.min),
                     reads=[R_smA[b], R_ns], writes=[R_smA[b]])
                P.op("act", lambda e: e.activation(
                    out=Pf[b][:, 0, w0:384], in_=psb[bk0][:, w0:384], func=AF.Exp, bias=smt[:, 2:3], scale=1.0, accum_out=smt[:, 4:5]),
                    reads=[PS[bk0], R_smA[b]], writes=[R_P[b], R_smB[b]])
                P.op("act", lambda e: e.activation(
                    out=Pf[b][:, 1, w0:384], in_=Pf[b][:, 1, w0:384], func=AF.Exp, bias=smt[:, 3:4], scale=1.0, accum_out=smt[:, 5:6]),
                    reads=[R_P[b], R_smA[b]], writes=[R_P[b], R_smB[b]])
                P.op("dve", lambda e: e.tensor_tensor(out=smt[:, 6:8], in0=sinkc[:, l * 8 + head0: l * 8 + head0 + 2], in1=smt[:, 2:4], op=ALU.add),
                     reads=[R_smA[b], R_sink], writes=[R_smC[b]])
                P.op("act", lambda e: e.activation(out=smt[:, 8:10], in_=smt[:, 6:8], func=AF.Exp), reads=[R_smC[b]], writes=[R_smC[b]])

            def w2(i, u):
                j = u["j"]
                b = i % NB
                wins = wins_of(j)
                w0 = wins[0][0] * 128
                smt = sm[b]
                ptp, Rptp = PTP[i % 2], R_PTP[i % 2]
                P.op("dve", lambda e: e.tensor_tensor(out=smt[:, 10:12], in0=smt[:, 4:6], in1=smt[:, 8:10], op=ALU.add), reads=[R_smB[b], R_smC[b]], writes=[R_smD[b]])
                P.op("dve", lambda e: e.reciprocal(out=smt[:, 12:14], in_=smt[:, 10:12]), reads=[R_smD[b]], writes=[R_smD[b]])
                for half in range(2):
                    P.op("dve", lambda e, half=half: e.tensor_scalar(out=Pn[b][:, half, w0:384], in0=Pf[b][:, half, w0:384],
                                                                     scalar1=smt[:, 12 + half:13 + half], scalar2=None, op0=ALU.mult),
                         reads=[R_P[b], R_smD[b]], writes=[R_Pn[b]])

            def w2_pe(i, u):
                j = u["j"]
                b = i % NB
                wins = wins_of(j)
                ptp, Rptp = PTP[i % 2], R_PTP[i % 2]
                for half in range(2):
                    for (w, r, sl) in wins:
                        P.op("pe", lambda e, half=half, w=w: e.transpose(out=ptp[:, half * 384 + w * 128: half * 384 + (w + 1) * 128],
                                                                         in_=Pn[b][:, half, w * 128:(w + 1) * 128], identity=ident),
                             reads=[R_Pn[b], R_cst], writes=[Rptp])

            def w2_act(i, u):
                j = u["j"]
                b = i % NB
                wins = wins_of(j)
                w0 = wins[0][0] * 128
                ptp, Rptp = PTP[i % 2], R_PTP[i % 2]
                src3 = ptp.rearrange("p (h n) -> p h n", h=2)[:, :, w0:384]
                dst3 = PTs[b].rearrange("p (h n) -> p h n", h=2)[:, :, w0:384]
                P.op("act", lambda e: e.activation(out=dst3, in_=src3, func=AF.Copy), reads=[Rptp], writes=[R_PT[b]])

            def w3(i, u):
                kv, j, jj, ob = u["kv"], u["j"], u["jj"], u["ob"]
                s, b = kv % 2, i % NB
                wins = wins_of(j)
                for half in range(2):
                    kw = dict(tile_position=(0, 64)) if half == 1 else {}
                    for wi, (w, r, sl) in enumerate(wins):
                        P.op("pe", lambda e, half=half, w=w, r=r, sl=sl, wi=wi, kw=kw: e.matmul(
                            psb[ob][64 * half:64 * half + 64, jj * 128:(jj + 1) * 128], lhsT=VV[s][:, r * 16 + sl, 0:64],
                            rhs=PTs[b][:, half * 384 + w * 128: half * 384 + (w + 1) * 128], start=False, stop=(wi == len(wins) - 1),
                            skip_group_check=True, **kw),
                            reads=[R_PT[b], RV[s]], writes=[PS[ob]])
                if jj == 3:
                    osel = u["osel"]
                    qrow = 4 + u["qi"]
                    P.op("act", lambda e: e.activation(out=osb[osel], in_=psb[ob][:, :], func=AF.Copy), reads=[PS[ob]], writes=[Rosb[osel]])
                    P.dma("act", ot_d[qrow * 128:(qrow + 1) * 128, gs(u["jg"])], osb[osel], reads=[Rosb[osel]], writes=[Rot_d], semres=Rosb[osel])

            R_dummySW = Res("dummySW")
            sw_kv_loads(0)
            sw_kv_loads(1)
            sw_q_load(0)
            sw_q_load(1)
            nsw = len(swu)
            w1(0, swu[0])
            for i in range(nsw + 3):
                if 0 <= i - 3 < nsw:
                    w3(i - 3, swu[i - 3])
                if i + 1 < nsw:
                    w1(i + 1, swu[i + 1])
                for dk in range(KEEP_WARM_SW):
                    tgt = pst.bitcast(F32)[:, 384:512] if dk % 2 == 0 else psb[4][:, 384:512]
                    P.op("pe", lambda e, tgt=tgt: e.matmul(tgt, lhsT=onesB, rhs=cst[:, 128:256], start=False, stop=True, skip_group_check=True),
                         reads=[R_cst], writes=[R_dummySW])
                if 0 <= i - 2 < nsw:
                    w2(i - 2, swu[i - 2])
                    w2_pe(i - 2, swu[i - 2])
                if i < nsw:
                    w1_rest(i, swu[i])
                if 0 <= i - 2 < nsw:
                    w2_act(i - 2, swu[i - 2])
            ar.reset(m)

        def phase_C(l):
            m = ar.mark()
            Rot_d = rd("ot")
            wload = make_wloader()
            big = ar.alloc(8 * NT, BF16).rearrange("p (c t) -> p c t", c=8)
            BG = [[Res("big%d_%d" % (c, g)) for g in range(NG)] for c in range(8)]
            NW = 6
            wp = [ar.alloc(8 * 256, BF16).rearrange("p (c n) -> p c n", c=8) for _ in range(NW)]
            Rw = [Res("wpC%d" % i) for i in range(NW)]
            T4, R_T4 = make_T4()
            w_in = wd["w_in"]
            reqs = []
            for pn in range(4):
                reqs += [(w_in[l, :, 2304 + pn * 256: 2304 + (pn + 1) * 256], 8), (w_in[l, :, 3328 + pn * 256: 3328 + (pn + 1) * 256], 8),
                         (wd["w_up_a"][l, :, pn * 256:(pn + 1) * 256], 4), (wd["w_up_b"][l, :, pn * 256:(pn + 1) * 256], 4)]
            for pn in range(4):
                reqs.append((wd["w_o"][l, :, pn * 256:(pn + 1) * 256], 8))
            for qf in range(4):
                for pn in range(4):
                    reqs.append((wd["w_ff1"][l, :, qf * 1024 + pn * 256: qf * 1024 + (pn + 1) * 256], 8))
                for pn in range(4):
                    reqs.append((wd["w_ff2"][l, qf * 1024:(qf + 1) * 1024, pn * 256:(pn + 1) * 256], 8))
            for pn in range(4):
                reqs += [(wd["w_pe"][l, :, pn * 256:(pn + 1) * 256], 2), (wd["w_pg"][l, :, pn * 256:(pn + 1) * 256], 8)]
            wst = {"issued": 0, "next": 0}

            def begin_step():
                upto = min(wst["next"] + NW, len(reqs))
                while wst["issued"] < upto:
                    r = wst["issued"]
                    src2, kc = reqs[r]
                    wload(wp[r % NW][:, 0:kc, :], Rw[r % NW], src2, kc, 256)
                    wst["issued"] += 1

            def getw(src2, kc):
                r = wst["next"]
                wst["next"] += 1
                assert r < wst["issued"] and reqs[r][1] == kc
                return wp[r % NW], Rw[r % NW]

            og = [ar.alloc(8 * 512, BF16).rearrange("p (k t) -> p k t", k=8) for _ in range(2)]
            Rog = [Res("og0"), Res("og1")]
            sg = [T4[0], T4[1]]
            m1 = [T4[2], T4[3]]
            Rsg = [R_T4[0], R_T4[1]]
            Rm1 = [R_T4[2], R_T4[3]]
            ot_v = ot_d.rearrange("(k p) t -> p k t", p=128)
            w_in = wd["w_in"]
            ogi = 0
            for pn in range(4):
                begin_step()
                wga, Rga = getw(w_in[l, :, 2304 + pn * 256: 2304 + (pn + 1) * 256], 8)
                wgb, Rgb = getw(w_in[l, :, 3328 + pn * 256: 3328 + (pn + 1) * 256], 8)
                wua, Rua = getw(wd["w_up_a"][l, :, pn * 256:(pn + 1) * 256], 4)
                wub, Rub = getw(wd["w_up_b"][l, :, pn * 256:(pn + 1) * 256], 4)
                for g in range(NG):
                    o = ogi % 2
                    ogi += 1
                    P.dma("sp", og[o], ot_v[:, :, gs(g)], reads=[Rot_d], writes=[Rog[o]])
                    for ocb in range(2):
                        oc = pn * 2 + ocb
                        cs = slice(ocb * 128, (ocb + 1) * 128)
                        for c in range(8):
                            P.op("pe", lambda e, c=c, g=g, cs=cs, wga=wga: e.matmul(psb[0][:, :], lhsT=wga[:, c, cs], rhs=hT[:, c, gs(g)], start=(c == 0), stop=(c == 7)),
                                 reads=[Rga, H[c][g]], writes=[PS[0]])
                        for c in range(4):
                            P.op("pe", lambda e, c=c, o=o, cs=cs, wua=wua: e.matmul(psb[1][:, :], lhsT=wua[:, c, cs], rhs=og[o][:, c, :], start=(c == 0), stop=(c == 3)),
                                 reads=[Rua, Rog[o]], writes=[PS[1]])
                        for c in range(8):
                            P.op("pe", lambda e, c=c, g=g, cs=cs, wgb=wgb: e.matmul(psb[2][:, :], lhsT=wgb[:, c, cs], rhs=hT[:, c, gs(g)], start=(c == 0), stop=(c == 7)),
                                 reads=[Rgb, H[c][g]], writes=[PS[2]])
                        for c in range(4):
                            P.op("pe", lambda e, c=c, o=o, cs=cs, wub=wub: e.matmul(psb[3][:, :], lhsT=wub[:, c, cs], rhs=og[o][:, 4 + c, :], start=(c == 0), stop=(c == 3)),
                                 reads=[Rub, Rog[o]], writes=[PS[3]])
                        P.op("act", lambda e: e.activation(out=sg[0], in_=psb[0][:, :], func=AF.Sigmoid), reads=[PS[0]], writes=[Rsg[0]])
                        P.op("dve", lambda e: e.tensor_tensor(out=m1[0], in0=sg[0], in1=psb[1][:, :], op=ALU.mult), reads=[Rsg[0], PS[1]], writes=[Rm1[0]])
                        P.op("act", lambda e: e.activation(out=sg[1], in_=psb[2][:, :], func=AF.Sigmoid), reads=[PS[2]], writes=[Rsg[1]])
                        P.op("dve", lambda e: e.tensor_tensor(out=m1[1], in0=sg[1], in1=psb[3][:, :], op=ALU.mult), reads=[Rsg[1], PS[3]], writes=[Rm1[1]])
                        P.op("dve", lambda e, oc=oc, g=g: e.tensor_tensor(out=big[:, oc, gs(g)], in0=m1[0], in1=m1[1], op=ALU.add),
                             reads=[Rm1[0], Rm1[1]], writes=[BG[oc][g]])
            ev = 0
            for pn in range(4):
                begin_step()
                wo, Ro = getw(wd["w_o"][l, :, pn * 256:(pn + 1) * 256], 8)
                for g in range(NG):
                    for ocb in range(2):
                        oc = pn * 2 + ocb
                        cs = slice(ocb * 128, (ocb + 1) * 128)
                        bank = 4 + ev % 3
                        ev += 1
                        for c in range(8):
                            P.op("pe", lambda e, c=c, g=g, cs=cs, wo=wo, bank=bank: e.matmul(psb[bank][:, :], lhsT=wo[:, c, cs], rhs=big[:, c, gs(g)], start=(c == 0), stop=(c == 7)),
                                 reads=[Ro, BG[c][g]], writes=[PS[bank]])
                        P.op("dve", lambda e, oc=oc, g=g, bank=bank: e.tensor_tensor(out=xT[:, oc, gs(g)], in0=xT[:, oc, gs(g)], in1=psb[bank][:, :], op=ALU.add),
                             reads=[PS[bank], X[oc][g]], writes=[X[oc][g]])
            rmsnorm(3 * l + 1, T4, R_T4)
            rl = [T4[0], T4[1]]
            Rrl = [R_T4[0], R_T4[1]]
            ev = 0
            for qf in range(4):
                for pn in range(4):
                    col0 = qf * 1024 + pn * 256
                    begin_step()
                    w1, R1 = getw(wd["w_ff1"][l, :, col0:col0 + 256], 8)
                    for g in range(NG):
                        for fb in range(2):
                            fc = pn * 2 + fb
                            cs = slice(fb * 128, (fb + 1) * 128)
                            bank = ev % 4
                            rs = ev % 2
                            ev += 1
                            for c in range(8):
                                P.op("pe", lambda e, c=c, g=g, cs=cs, w1=w1, bank=bank: e.matmul(psb[bank][:, :], lhsT=w1[:, c, cs], rhs=hT[:, c, gs(g)], start=(c == 0), stop=(c == 7)),
                                     reads=[R1, H[c][g]], writes=[PS[bank]])
                            P.op("act", lambda e, rs=rs, bank=bank: e.activation(out=rl[rs], in_=psb[bank][:, :], func=AF.Relu),
                                 reads=[PS[bank]], writes=[Rrl[rs]])
                            P.op("dve", lambda e, rs=rs, fc=fc, g=g: e.tensor_tensor(out=big[:, fc, gs(g)], in0=rl[rs], in1=rl[rs], op=ALU.mult),
                                 reads=[Rrl[rs]], writes=[BG[fc][g]])
                for pn in range(4):
                    begin_step()
                    w2, R2 = getw(wd["w_ff2"][l, qf * 1024:(qf + 1) * 1024, pn * 256:(pn + 1) * 256], 8)
                    for g in range(NG):
                        for ocb in range(2):
                            oc = pn * 2 + ocb
                            cs = slice(ocb * 128, (ocb + 1) * 128)
                            bank = 4 + ev % 3
                            ev += 1
                            for c in range(8):
                                P.op("pe", lambda e, c=c, g=g, cs=cs, w2=w2, bank=bank: e.matmul(psb[bank][:, :], lhsT=w2[:, c, cs], rhs=big[:, c, gs(g)], start=(c == 0), stop=(c == 7)),
                                     reads=[R2, BG[c][g]], writes=[PS[bank]])
                            P.op("dve", lambda e, oc=oc, g=g, bank=bank: e.tensor_tensor(out=xT[:, oc, gs(g)], in0=xT[:, oc, gs(g)], in1=psb[bank][:, :], op=ALU.add),
                                 reads=[PS[bank], X[oc][g]], writes=[X[oc][g]])
            rmsnorm(3 * l + 2, T4, R_T4)
            pTb = big[:, 0:2, :]
            for c in range(2):
                for g in range(NG):
                    wload(pTb[:, c:c + 1, gs(g)], BG[c][g], pT_d[l, c * 128:(c + 1) * 128, gs(g)], 1, 512)
            ev = 0
            for pn in range(4):
                begin_step()
                wpe, Rpe = getw(wd["w_pe"][l, :, pn * 256:(pn + 1) * 256], 2)
                wpg, Rpg = getw(wd["w_pg"][l, :, pn * 256:(pn + 1) * 256], 8)
                for g in range(NG):
                    for ocb in range(2):
                        oc = pn * 2 + ocb
                        cs = slice(ocb * 128, (ocb + 1) * 128)
                        b0 = (ev % 2) * 2
                        ss = ev % 2
                        ev += 1
                        for c in range(2):
                            P.op("pe", lambda e, c=c, g=g, cs=cs, wpe=wpe, b0=b0: e.matmul(psb[b0][:, :], lhsT=wpe[:, c, cs], rhs=pTb[:, c, gs(g)], start=(c == 0), stop=(c == 1)),
                                 reads=[Rpe, BG[c][g]], writes=[PS[b0]])
                        for c in range(8):
                            P.op("pe", lambda e, c=c, g=g, cs=cs, wpg=wpg, b0=b0: e.matmul(psb[b0 + 1][:, :], lhsT=wpg[:, c, cs], rhs=hT[:, c, gs(g)], start=(c == 0), stop=(c == 7)),
                                 reads=[Rpg, H[c][g]], writes=[PS[b0 + 1]])
                        P.op("act", lambda e, ss=ss, b0=b0: e.activation(out=sg[ss], in_=psb[b0 + 1][:, :], func=AF.Sigmoid), reads=[PS[b0 + 1]], writes=[Rsg[ss]])
                        P.op("dve", lambda e, ss=ss, b0=b0: e.tensor_tensor(out=m1[ss], in0=sg[ss], in1=psb[b0][:, :], op=ALU.mult), reads=[Rsg[ss], PS[b0]], writes=[Rm1[ss]])
                        P.op("dve", lambda e, ss=ss, oc=oc, g=g: e.tensor_tensor(out=xT[:, oc, gs(g)], in0=xT[:, oc, gs(g)], in1=m1[ss], op=ALU.add),
                             reads=[Rm1[ss], X[oc][g]], writes=[X[oc][g]])
            ar.reset(m)

        for sidx, stp in enumerate(steps):
            if stp[0] == "A":
                phase_A(stp[1])
            elif stp[0] == "B":
                phase_B(stp[1])
            elif stp[0] == "C":
                phase_C(stp[1])
            elif stp[0] == "F":
                T4f, R_T4f = make_T4()
                rmsnorm(6, T4f, R_T4f, out_final=out_d)
            P.barrier()
        if xout_d is not None:
            xo_v = xout_d.rearrange("(c p) t -> p c t", p=128)
            for c in range(8):
                P.dma("sp", xo_v[:, c, :], xT[:, c, :], reads=[X[c][g] for g in range(NG)], writes=[rd("xout")])
        P.barrier()
        P.emit()
    return nc


def _t5_bucket(dist):
    max_exact = 16
    d = np.maximum(dist, 0)
    df = np.maximum(d, 1).astype(np.float32)
    large = max_exact + (np.log(df / np.float32(max_exact)) / np.float32(np.log(128 / max_exact))
                         * np.float32(32 - max_exact)).astype(np.int32)
    large = np.minimum(large, 31)
    return np.where(d < max_exact, d, large)


def _consts():
    j = np.arange(128)[:, None]
    s = np.arange(128)[None, :]
    ident = (j == s).astype(np.float32)
    negU = np.where(j >= s, -1.0, 0.0).astype(np.float32)
    ones = np.ones((128, 128), np.float32)
    onesM = np.full((128, 128), 1.0 / 1024.0, np.float32)
    return np.ascontiguousarray(np.concatenate([ident, negU, ones, onesM], axis=1))


def _own_tokens(a, hf):
    F = a.shape[-1]
    return a.reshape(16, 2, 128, F)[:, hf].reshape(2048, F)


def _common_maps(inputs):
    g_list = [inputs["g_mix"][0], inputs["g_mlp"][0], inputs["g_pe"][0],
              inputs["g_mix"][1], inputs["g_mlp"][1], inputs["g_pe"][1], inputs["g_final"]]
    gcols = np.zeros((128, 56), np.float32)
    for jx, gv in enumerate(g_list):
        gcols[:, jx * 8:(jx + 1) * 8] = np.asarray(gv, np.float32).reshape(8, 128).T
    sink = np.asarray(inputs["sinks"], np.float32).reshape(1, 16)
    sinkcol = np.ascontiguousarray(np.broadcast_to(sink, (128, 16)))
    i = np.arange(128)[:, None]
    jj = np.arange(256)[None, :]
    dist = 128 + i - jj
    band = (dist >= 0) & (dist < 128)
    bucket = _t5_bucket(dist)
    rb = np.asarray(inputs["rel_bias"], np.float32)
    Bg = rb[bucket]
    Bg = np.where(band[:, :, None], Bg, np.float32(0.0)).transpose(2, 0, 1)
    bandmask = np.where(band, np.float32(0.0), np.float32(NEG))
    per_hf = []
    diag = np.where(np.arange(128)[:, None] >= np.arange(128)[None, :], np.float32(NEG), np.float32(0.0))
    for hf in range(2):
        biasg = np.zeros((8, 128, 384), np.float32)
        maskc = np.full((128, 384), NEG, np.float32)
        o = 0 if hf == 0 else 128
        biasg[:, :, o:o + 256] = Bg
        maskc[:, o:o + 256] = bandmask
        biasg = np.ascontiguousarray(biasg.transpose(1, 0, 2).reshape(128, 8 * 384))
        if hf == 0:
            sbmask = np.concatenate([diag, np.full((128, 128), NEG, np.float32)], axis=1)
        else:
            sbmask = np.concatenate([np.zeros((128, 128), np.float32), diag], axis=1)
        per_hf.append(dict(biasg=biasg, maskc=maskc, sbmask=np.ascontiguousarray(sbmask)))
    return dict(consts=_consts(), gcols=gcols, sinkcol=sinkcol), per_hf


_NC_CACHE = {}


def _get_nc(key, steps, fused=False):
    if key not in _NC_CACHE:
        _NC_CACHE[key] = build(steps, fused=fused)
    return _NC_CACHE[key]


def _f32(a):
    return np.ascontiguousarray(np.asarray(a, np.float32))


FUSED = True


def _assemble(r3):
    out = np.zeros((4, 4096, 1024), np.float32)
    ov = out.reshape(4, 16, 2, 128, 1024)
    for c in range(8):
        b, hf = c // 2, c % 2
        ov[b, :, hf] = np.asarray(r3[c]["outT"], np.float32).T.reshape(16, 128, 1024)
    return out


def kernel(**inputs):
    inputs = {k: np.asarray(v) for k, v in inputs.items()}
    x = _f32(inputs["x"])
    p = _f32(inputs["p"])
    common, per_hf = _common_maps(inputs)
    W = {k: _f32(inputs[k]) for k in W_SHAPES}
    cores = list(range(8))

    def base_map(c):
        b, hf = c // 2, c % 2
        mp = dict(common)
        return mp, b, hf

    xT = []
    pT = []
    for c in cores:
        b, hf = c // 2, c % 2
        xT.append(np.ascontiguousarray(_own_tokens(x[b], hf).T))
        pT.append(np.ascontiguousarray(np.stack([_own_tokens(p[l, b], hf).T for l in range(2)])))

    def gather_pairs(res, name):
        outs = []
        for c in cores:
            b = c // 2
            outs.append(np.ascontiguousarray(np.concatenate([res[2 * b][name], res[2 * b + 1][name]], axis=0)))
        return outs

    def gathered(res, l):
        d = {}
        for pt in range(2):
            d["kxall%d_%d" % (l, pt)] = gather_pairs(res, "kx%d_%d" % (l, pt))
            d["vxall%d_%d" % (l, pt)] = gather_pairs(res, "vx%d_%d" % (l, pt))
        return d

    if FUSED:
        ncf = _get_nc("FUSED", [("A", 0), ("B", 0), ("C", 0), ("A", 1), ("B", 1), ("C", 1), ("F",)], fused=True)
        maps = []
        for c in cores:
            mp, b, hf = base_map(c)
            mp.update(xin=xT[c], pT=pT[c], **per_hf[hf], **W)
            maps.append(mp)
        r3 = run_bass_kernel_spmd(ncf, maps, core_ids=cores).results
        return _assemble(r3)

    nc1 = _get_nc("L1", [("A", 0)])
    maps = []
    for c in cores:
        mp, b, hf = base_map(c)
        mp.update(xin=xT[c], w_in=W["w_in"])
        maps.append(mp)
    r1 = run_bass_kernel_spmd(nc1, maps, core_ids=cores).results
    nc2 = _get_nc("L2", [("B", 0), ("C", 0), ("A", 1)])
    gth = gathered(r1, 0)
    maps = []
    for c in cores:
        mp, b, hf = base_map(c)
        mp.update(xin=xT[c], pT=pT[c], q0=r1[c]["q0"], **{k: v[c] for k, v in gth.items()}, **per_hf[hf], **W)
        maps.append(mp)
    r2 = run_bass_kernel_spmd(nc2, maps, core_ids=cores).results
    nc3 = _get_nc("L3", [("B", 1), ("C", 1), ("F",)])
    gth = gathered(r2, 1)
    maps = []
    for c in cores:
        mp, b, hf = base_map(c)
        mp.update(xin=r2[c]["xout"], pT=pT[c], q1=r2[c]["q1"], **{k: v[c] for k, v in gth.items()}, **per_hf[hf], **W)
        maps.append(mp)
    r3 = run_bass_kernel_spmd(nc3, maps, core_ids=cores).results
    return _assemble(r3)
```
